# Optimizing a Trainium2 kernel written in Bass

```python
import math
import jax, jax.numpy as jnp
from jax import lax
import numpy as np

D_MODEL = 1024
BATCH = 16
SEQ = 2048
DEPTH = 1

BLOCK = 128
EPS = 1e-6
HEAD_DIM = 64
SWA_Q_HEADS = 8
SWA_KV_HEADS = 2
SWA_WINDOW = 128
N_BUCKETS = 32
MAX_DISTANCE = 128
SB_HEADS = 8
MEM_LEN = 256
MEM_HEADS = 4
MEM_HEAD_DIM = 128
SWA_Q_W = SWA_Q_HEADS * HEAD_DIM
SWA_KV_W = SWA_KV_HEADS * HEAD_DIM
SB_W = SB_HEADS * HEAD_DIM
MEM_W = MEM_HEADS * MEM_HEAD_DIM
N_BRANCH = 3
IN_SPLITS = (SWA_Q_W, SWA_KV_W, SWA_KV_W, SB_W, SB_W, SB_W, MEM_W, N_BRANCH * D_MODEL)
IN_W = sum(IN_SPLITS)
D_FF = -(-8 * D_MODEL // (3 * 256)) * 256

kernel_name = "hybrid_gated_swa_stickbreak_memxattn_swiglu"


def rms_norm(x, g):
    xf = x.astype(jnp.float32)
    y = xf * lax.rsqrt(jnp.mean(xf * xf, axis=-1, keepdims=True) + EPS)
    return (y * g.astype(jnp.float32)).astype(x.dtype)


def t5_bucket(dist):
    max_exact = N_BUCKETS // 2
    d = jnp.maximum(dist, 0)
    df = jnp.maximum(d, 1).astype(jnp.float32)
    large = max_exact + (jnp.log(df / max_exact) / math.log(MAX_DISTANCE / max_exact)
                         * (N_BUCKETS - max_exact)).astype(jnp.int32)
    large = jnp.minimum(large, N_BUCKETS - 1)
    return jnp.where(d < max_exact, d, large)


def swa_sink_attention(q, k, v, sinks, rel_bias):
    B, S, Hq, d = q.shape
    Hkv = k.shape[2]
    G = Hq // Hkv
    nb = S // BLOCK
    qb = q.reshape(B, nb, BLOCK, Hkv, G, d)
    kb = k.reshape(B, nb, BLOCK, Hkv, d)
    vb = v.reshape(B, nb, BLOCK, Hkv, d)
    kband = jnp.concatenate([jnp.concatenate([jnp.zeros_like(kb[:, :1]), kb[:, :-1]], axis=1), kb], axis=2)
    vband = jnp.concatenate([jnp.concatenate([jnp.zeros_like(vb[:, :1]), vb[:, :-1]], axis=1), vb], axis=2)
    scores = jnp.einsum('bnqhgd,bnkhd->bnhgqk', qb, kband).astype(jnp.float32) * (d ** -0.5)
    dist = (jnp.arange(BLOCK)[:, None] + BLOCK) - jnp.arange(2 * BLOCK)[None, :]
    in_win = (dist >= 0) & (dist < SWA_WINDOW)
    bias = rel_bias.astype(jnp.float32)[t5_bucket(dist)]
    bias = bias.transpose(2, 0, 1).reshape(Hkv, G, BLOCK, 2 * BLOCK)
    k_abs = (jnp.arange(nb)[:, None] - 1) * BLOCK + jnp.arange(2 * BLOCK)[None, :]
    mask = in_win[None] & (k_abs >= 0)[:, None, :]
    scores = jnp.where(mask[None, :, None, None], scores + bias[None, None], -jnp.inf)
    sink = sinks.astype(jnp.float32).reshape(Hkv, G)[:, :, None, None]
    m = jnp.maximum(jnp.max(scores, axis=-1, keepdims=True), sink)
    p = jnp.exp(scores - m)
    w = p / (jnp.sum(p, axis=-1, keepdims=True) + jnp.exp(sink - m))
    out = jnp.einsum('bnhgqk,bnkhd->bnqhgd', w.astype(v.dtype), vband)
    return out.reshape(B, S, Hq * d)


def stick_breaking_attention(q, k, v):
    B, S, H, d = q.shape
    nb = S // BLOCK
    outs = []
    for i in range(nb):
        L = (i + 1) * BLOCK
        z = jnp.einsum('bqhd,bkhd->bhqk', q[:, i * BLOCK:L], k[:, :L]).astype(jnp.float32) * (d ** -0.5)
        t = i * BLOCK + jnp.arange(BLOCK)[:, None]
        causal = jnp.arange(L)[None, :] < t
        log_1m = jnp.where(causal, jax.nn.log_sigmoid(-z), 0.0)
        between = lax.cumsum(log_1m, axis=3, reverse=True) - log_1m
        a = jnp.where(causal, jnp.exp(jax.nn.log_sigmoid(z) + between), 0.0)
        outs.append(jnp.einsum('bhqk,bkhd->bqhd', a.astype(v.dtype), v[:, :L]))
    return jnp.concatenate(outs, axis=1).reshape(B, S, H * d)


def memory_cross_attention(q, mk, mv):
    B, S, H, d = q.shape
    z = jnp.einsum('bshd,bmhd->bhsm', q, mk).astype(jnp.float32) * (d ** -0.5)
    w = jax.nn.softmax(z, axis=-1)
    return jnp.einsum('bhsm,bmhd->bshd', w.astype(mv.dtype), mv).reshape(B, S, H * d)


def setup_inputs(seed: int = 0) -> dict:
    key = jax.random.key(seed)
    ks = jax.random.split(key, 20)
    f = jnp.float32

    def w(k, shape, fan_in):
        return jax.random.normal(k, shape, f) * fan_in ** -0.5

    def gain(k, n):
        return 1.0 + 0.01 * jax.random.normal(k, (DEPTH, n), f)

    return {
        "x": jax.random.normal(ks[0], (BATCH, SEQ, D_MODEL), f),
        "mem": jax.random.normal(ks[1], (BATCH, MEM_LEN, D_MODEL), f),
        "ln_mix_pre": gain(ks[2], D_MODEL),
        "ln_mix_post": gain(ks[3], D_MODEL),
        "w_in": w(ks[4], (DEPTH, D_MODEL, IN_W), D_MODEL),
        "swa_sinks": 0.5 * jax.random.normal(ks[5], (DEPTH, SWA_Q_HEADS), f),
        "rel_bias": 0.5 * jax.random.normal(ks[6], (N_BUCKETS, SWA_Q_HEADS), f),
        "ln_mem": gain(ks[7], D_MODEL),
        "w_mem_kv": w(ks[8], (DEPTH, D_MODEL, 2 * MEM_W), D_MODEL),
        "w_branch_swa": w(ks[9], (DEPTH, SWA_Q_W, D_MODEL), SWA_Q_W),
        "w_branch_sb": w(ks[10], (DEPTH, SB_W, D_MODEL), SB_W),
        "w_branch_mem": w(ks[11], (DEPTH, MEM_W, D_MODEL), MEM_W),
        "w_out": w(ks[12], (DEPTH, D_MODEL, D_MODEL), D_MODEL),
        "ln_ffn_pre": gain(ks[13], D_MODEL),
        "ln_ffn_post": gain(ks[14], D_MODEL),
        "w_gate": w(ks[15], (DEPTH, D_MODEL, D_FF), D_MODEL),
        "w_up": w(ks[16], (DEPTH, D_MODEL, D_FF), D_MODEL),
        "w_down": w(ks[17], (DEPTH, D_FF, D_MODEL), D_FF),
    }


def reference(x, mem, ln_mix_pre, ln_mix_post, w_in, swa_sinks, rel_bias, ln_mem, w_mem_kv,
              w_branch_swa, w_branch_sb, w_branch_mem, w_out, ln_ffn_pre, ln_ffn_post,
              w_gate, w_up, w_down):
    B, S, D = x.shape
    M = mem.shape[1]
    split_idx = list(np.cumsum(IN_SPLITS)[:-1])
    h = x
    for l in range(DEPTH):
        u = rms_norm(h, ln_mix_pre[l])
        proj = jnp.einsum('bsd,de->bse', u, w_in[l])
        qa, ka, va, qb, kb, vb, qm, gl = jnp.split(proj, split_idx, axis=-1)
        y_swa = swa_sink_attention(qa.reshape(B, S, SWA_Q_HEADS, HEAD_DIM),
                                   ka.reshape(B, S, SWA_KV_HEADS, HEAD_DIM),
                                   va.reshape(B, S, SWA_KV_HEADS, HEAD_DIM),
                                   swa_sinks[l], rel_bias)
        y_sb = stick_breaking_attention(qb.reshape(B, S, SB_HEADS, HEAD_DIM),
                                        kb.reshape(B, S, SB_HEADS, HEAD_DIM),
                                        vb.reshape(B, S, SB_HEADS, HEAD_DIM))
        mkv = jnp.einsum('bmd,de->bme', rms_norm(mem, ln_mem[l]), w_mem_kv[l])
        mk, mv = jnp.split(mkv, 2, axis=-1)
        y_mem = memory_cross_attention(qm.reshape(B, S, MEM_HEADS, MEM_HEAD_DIM),
                                       mk.reshape(B, M, MEM_HEADS, MEM_HEAD_DIM),
                                       mv.reshape(B, M, MEM_HEADS, MEM_HEAD_DIM))
        g = jax.nn.sigmoid(gl.reshape(B, S, N_BRANCH, D))
        merged = (g[:, :, 0] * jnp.einsum('bse,ed->bsd', y_swa, w_branch_swa[l])
                  + g[:, :, 1] * jnp.einsum('bse,ed->bsd', y_sb, w_branch_sb[l])
                  + g[:, :, 2] * jnp.einsum('bse,ed->bsd', y_mem, w_branch_mem[l]))
        mix = jnp.einsum('bsd,de->bse', merged, w_out[l])
        h = h + rms_norm(mix, ln_mix_post[l])
        u = rms_norm(h, ln_ffn_pre[l])
        a = jax.nn.silu(jnp.einsum('bsd,df->bsf', u, w_gate[l])) * jnp.einsum('bsd,df->bsf', u, w_up[l])
        ffn = jnp.einsum('bsf,fd->bsd', a, w_down[l])
        h = h + rms_norm(ffn, ln_ffn_post[l])
    return h
```

```python
import math
import contextlib
import numpy as np
import concourse.bass as bass
import concourse.mybir as mybir
from concourse.bass_utils import run_bass_kernel_spmd

F32 = mybir.dt.float32
BF16 = mybir.dt.bfloat16
AF = mybir.ActivationFunctionType
ALU = mybir.AluOpType

NCORES = 8
NSEQ = 2
S = 2048
D = 1024
NBLK = 16
DFF = 2816
NFC = 22
MEM = 256
EPS = 1e-6
QA, KA, VA, QB, KB, VB, QM, GL = 0, 512, 640, 768, 1280, 1792, 2304, 2816

SAME_ENGINE_SYNC = True


class Tok:
    __slots__ = ("name", "w", "r")

    def __init__(self, name):
        self.name = name
        self.w = None
        self.r = []


class Op:
    __slots__ = ("id", "eng", "fn", "deps", "is_dma", "dsem", "dval", "needs_inc", "cnt")


class _Rec:
    def __init__(self):
        self.call = None

    def __getattr__(self, name):
        def f(*a, **k):
            self.call = (name, a, k)
            return self
        return f

    def replay(self, eng):
        name, a, k = self.call
        return getattr(eng, name)(*a, **k)


class Prog:
    ENGS = ("pe", "act", "dve", "pool", "sp")

    def __init__(self, nc):
        self.nc = nc
        self.ops = []
        self.by_eng = {e: [] for e in self.ENGS}
        self.dma_streams = {}

    def add(self, eng, fn, reads=(), writes=(), dma=None):
        op = Op()
        op.id = len(self.ops)
        op.eng = eng
        rec = _Rec()
        fn(rec)
        op.fn = rec.replay
        op.is_dma = dma is not None
        op.needs_inc = False
        op.cnt = 0
        deps = set()
        for t in reads:
            if t.w is not None:
                deps.add(t.w)
        for t in writes:
            if t.w is not None:
                deps.add(t.w)
            deps.update(t.r)
        op.deps = deps
        for t in reads:
            t.r.append(op.id)
        for t in writes:
            t.w = op.id
            t.r = []
        if dma is not None:
            if dma not in self.dma_streams:
                self.dma_streams[dma] = [len(self.dma_streams), 0]
            st = self.dma_streams[dma]
            st[1] += 16
            op.dsem = st[0]
            op.dval = st[1]
        self.ops.append(op)
        self.by_eng[eng].append(op)
        return op

    def pe(self, fn, reads=(), writes=()):
        return self.add("pe", fn, reads, writes)

    def act(self, fn, reads=(), writes=()):
        return self.add("act", fn, reads, writes)

    def dve(self, fn, reads=(), writes=()):
        return self.add("dve", fn, reads, writes)

    def pool(self, fn, reads=(), writes=()):
        return self.add("pool", fn, reads, writes)

    def dma(self, eng, stream, fn, reads=(), writes=()):
        return self.add(eng, fn, reads, writes, dma=stream)

    def emit(self, final_wait_streams=()):
        nc = self.nc
        ops = self.ops
        for op in ops:
            for d in op.deps:
                dop = ops[d]
                if dop.is_dma:
                    continue
                if dop.eng != op.eng or (SAME_ENGINE_SYNC and op.eng != "pe"):
                    dop.needs_inc = True
        for e in self.ENGS:
            c = 0
            for op in self.by_eng[e]:
                if op.needs_inc:
                    c += 1
                op.cnt = c
        with contextlib.ExitStack() as es:
            esem = {e: es.enter_context(nc.semaphore(f"s_{e}")) for e in self.ENGS}
            dsem = [es.enter_context(nc.semaphore(f"d_{i}")) for i in range(len(self.dma_streams))]
            block = es.enter_context(nc.Block())
            streams_final = [tuple(self.dma_streams[s]) for s in final_wait_streams]

            def make(e):
                def body(eng):
                    known = {}
                    for op in self.by_eng[e]:
                        waits = {}
                        for d in op.deps:
                            dop = ops[d]
                            if dop.is_dma:
                                key = ("d", dop.dsem)
                                val = dop.dval
                            else:
                                if dop.eng == e and (e == "pe" or not SAME_ENGINE_SYNC):
                                    continue
                                key = ("e", dop.eng)
                                val = dop.cnt
                            if waits.get(key, 0) < val:
                                waits[key] = val
                        for key, val in waits.items():
                            if known.get(key, 0) >= val:
                                continue
                            known[key] = val
                            sem = dsem[key[1]] if key[0] == "d" else esem[key[1]]
                            eng.wait_ge(sem, val)
                        inst = op.fn(eng)
                        if op.is_dma:
                            inst.then_inc(dsem[op.dsem], 16)
                        elif op.needs_inc:
                            inst.then_inc(esem[e], 1)
                    if e == "sp":
                        for (si, val) in streams_final:
                            eng.wait_ge(dsem[si], val)
                return body

            block.tensor(make("pe"))
            block.scalar(make("act"))
            block.vector(make("dve"))
            block.gpsimd(make("pool"))
            block.sync(make("sp"))


class _Stop(Exception):
    pass


class Mem:
    def __init__(self):
        self.reg = []

    def tok(self, name, lo, hi):
        t = Tok(name)
        seen = set()
        for (a, b, o) in self.reg:
            if a < hi and lo < b:
                if o.w is not None:
                    seen.add(o.w)
                seen.update(o.r)
        t.r = list(seen)
        self.reg.append((lo, hi, t))
        return t


def _t5_bucket_np(dist):
    max_exact = 16
    d = np.maximum(dist, 0)
    df = np.maximum(d, 1).astype(np.float32)
    large = max_exact + (np.log(df / np.float32(max_exact)) / np.float32(math.log(128 / max_exact))
                         * np.float32(32 - max_exact)).astype(np.int32)
    large = np.minimum(large, 31)
    return np.where(d < max_exact, d, large)


def _host_consts():
    k = np.arange(128)[:, None]
    q = np.arange(128)[None, :]
    ident = np.eye(128, dtype=np.float32)
    negtri = -(k >= q).astype(np.float32)
    negones = -np.ones((128, 128), np.float32)
    ones = np.ones((128, 128), np.float32)
    caus = (k < q).astype(np.float32)
    negm = np.where(k >= q, -30000.0, 0.0).astype(np.float32)
    cst = np.stack([ident, negtri, negones, ones, caus, negm], axis=1).reshape(128, 6 * 128)
    sel = np.zeros((128, 32, 2, 128), np.float32)
    for kbi in range(2):
        dist = (q + 128 - k) if kbi == 0 else (q - k)
        valid = (dist >= 0) & (dist < 128)
        bucket = _t5_bucket_np(dist)
        for b in range(32):
            sel[:, b, kbi, :] = (valid & (bucket == b)).astype(np.float32)
    return np.ascontiguousarray(cst), np.ascontiguousarray(sel.reshape(128, 32 * 256))


def build_nc(nseq=NSEQ, stop_after=None, dbg=None):
    nc = bass.Bass("TRN2", target_bir_lowering=False)

    def din(name, shape):
        return nc.dram_tensor(name, shape, F32, kind="ExternalInput").ap()

    x = din("x", [nseq, S, D])
    mem = din("mem", [nseq, MEM, D])
    w_in = din("w_in", [D, 5888])
    w_mem_kv = din("w_mem_kv", [D, 1024])
    w_b = [din("w_bswa", [512, D]), din("w_bsb", [512, D]), din("w_bmem", [512, D])]
    w_out = din("w_out", [D, D])
    w_gate = din("w_gate", [D, DFF])
    w_up = din("w_up", [D, DFF])
    w_down = din("w_down", [DFF, D])
    gains = din("gains", [5, D])
    sinks = din("sinks", [2, 4])
    relb = din("relb", [1, 256])
    cst = din("cst", [128, 768])
    sel = din("sel", [128, 8192])
    out = nc.dram_tensor("out", [nseq, S, D], F32, kind="ExternalOutput").ap()

    P = Prog(nc)
    M = Mem()
    NU = 176128 // 2
    A0 = 32768

    with contextlib.ExitStack() as es:
        def sb(name, shape, dt):
            return es.enter_context(nc.sbuf_tensor(name, shape, dt))

        cb = sb("cb", [128, 6, 128], BF16)
        gB = sb("gB", [128, 4, 1024], F32)
        EB = sb("EB", [128, 4, 512], F32)
        Eb = sb("Eb", [128, 256], F32)
        sinkexp = sb("sinkexp", [128, 4], F32)
        mhalf = sb("mhalf", [128, 1], F32)
        eps_t = sb("eps_t", [128, 1], F32)
        st = sb("st", [128, 512], F32)
        junk = sb("junk", [128, 1024], BF16)
        jk2 = sb("jk2", [128, 2, 1024], BF16)
        t_jk = [Tok("jk0"), Tok("jk1")]
        U = sb("U", [128, NU], BF16)
        ps = es.enter_context(nc.psum_tensor("ps", [128, 8, 512], F32))
        uT = U[:, 0:16384].rearrange("p (k n) -> p k n", k=8)

        t_cb, t_EB, t_Eb, t_sink, t_mhalf, t_junk = (Tok(n) for n in ("cb", "EB", "Eb", "sink", "mhalf", "junk"))
        t_gB = [Tok(f"gB{i}") for i in range(4)]
        pb = [Tok(f"pb{i}") for i in range(8)]
        IDENT, NEGTRI, NEGONES, ONES, CAUS, NEGM = (cb[:, i, :] for i in range(6))
        statcol = [0]

        def uview(lo, nbytes, dt, pattern=None, **kw):
            ap = U[:, lo // 2:(lo + nbytes) // 2]
            if dt == F32:
                ap = ap.bitcast(F32)
            if pattern is not None:
                ap = ap.rearrange(pattern, **kw)
            return ap

        def ualloc(name, lo, nbytes, dt, pattern=None, **kw):
            return uview(lo, nbytes, dt, pattern, **kw), M.tok(name, lo, lo + nbytes)

        def psT(bank):
            return ps[:, bank, :].bitcast(BF16)

        def mm(out_ap, lhsT, rhs, start, stop, reads, writes, skip=False):
            P.pe(lambda e: e.matmul(out_ap, lhsT=lhsT, rhs=rhs, start=start, stop=stop, skip_group_check=skip),
                 reads, writes)

        def load_w(stream, dst_view, src_ap, tok, after=()):
            P.dma("pool", stream, lambda e: e.dma_start(out=dst_view, in_=src_ap), reads=list(after), writes=[tok])

        def rstd_of(src_ap, src_toks, junk_view):
            c = statcol[0]
            statcol[0] += 3
            tk = Tok(f"st{c}")
            P.act(lambda e: e.activation(out=junk_view, in_=src_ap, func=AF.Square, accum_out=st[:, c:c + 1]),
                  reads=src_toks, writes=[tk, t_junk])
            P.act(lambda e: e.activation(out=st[:, c + 1:c + 2], in_=st[:, c:c + 1], func=AF.Ln, scale=1.0 / D, bias=eps_t[:, 0:1]),
                  reads=[tk, t_mhalf], writes=[tk])
            P.act(lambda e: e.activation(out=st[:, c + 2:c + 3], in_=st[:, c + 1:c + 2], func=AF.Exp, scale=-0.5),
                  reads=[tk], writes=[tk])
            return st[:, c + 2:c + 3], tk

        junk2 = junk[:].rearrange("p (a b) -> p a b", a=2)
        jrr = [0]

        def rstd_stages(src_ap, src_toks, paired):
            c = statcol[0]
            statcol[0] += 3
            tk = Tok(f"st{c}")
            ji = jrr[0] % 2
            jrr[0] += 1
            jv = jk2[:, ji, :]
            if paired:
                jv = jv.rearrange("p (a b) -> p a b", a=2)

            def a():
                P.act(lambda e: e.activation(out=jv, in_=src_ap, func=AF.Square, accum_out=st[:, c:c + 1]),
                      reads=src_toks, writes=[tk, t_jk[ji]])

            def b():
                P.act(lambda e: e.activation(out=st[:, c + 1:c + 2], in_=st[:, c:c + 1], func=AF.Ln, scale=1.0 / D, bias=eps_t[:, 0:1]),
                      reads=[tk, t_mhalf], writes=[tk])

            def cc():
                P.act(lambda e: e.activation(out=st[:, c + 2:c + 3], in_=st[:, c + 1:c + 2], func=AF.Exp, scale=-0.5),
                      reads=[tk], writes=[tk])

            return [a, b, cc], st[:, c + 2:c + 3], tk

        P.dma("pool", "cst", lambda e: e.dma_start(out=cb[:].rearrange("p a b -> p (a b)"), in_=cst), writes=[t_cb])
        for i in range(4):
            P.dma("act", f"g{i}", lambda e, i=i: e.dma_start(out=gB[:, i, :], in_=gains[i, :].partition_broadcast(128)),
                  writes=[t_gB[i]])
        P.dma("act", "relb", lambda e: e.dma_start(out=Eb[:], in_=relb[0, :].partition_broadcast(128)), writes=[t_Eb])
        for j in range(2):
            P.dma("act", f"sink{j}", lambda e, j=j: e.dma_start(out=sinkexp[j * 64:(j + 1) * 64, :],
                                                               in_=sinks[j, :].partition_broadcast(64)), writes=[t_sink])
        SELv, t_sel = ualloc("sel", 126976, 16384, BF16, "p (b n) -> p b n", b=32)
        P.pool(lambda e: e.memset(eps_t[:], EPS), writes=[t_mhalf])
        P.act(lambda e: e.activation(out=Eb[:], in_=Eb[:], func=AF.Exp), reads=[t_Eb], writes=[t_Eb])
        P.act(lambda e: e.activation(out=sinkexp[:], in_=sinkexp[:], func=AF.Exp), reads=[t_sink], writes=[t_sink])
        t_EBh = [Tok(f"EBh{h}") for h in range(8)]
        bg_dve = []
        for b in range(32):
            for c in range(4):
                for j in range(2):
                    h = 2 * c + j
                    ebv = EB[:, c, j * 256:(j + 1) * 256]
                    sc = Eb[:, b * 8 + h:b * 8 + h + 1]
                    if b == 0:
                        bg_dve.append(lambda ebv=ebv, sc=sc, h=h: P.dve(
                            lambda e: e.tensor_scalar(out=ebv, in0=SELv[:, 0, :], scalar1=sc, scalar2=None, op0=ALU.mult),
                            reads=[t_sel, t_Eb], writes=[t_EBh[h]]))
                    else:
                        bg_dve.append(lambda ebv=ebv, sc=sc, h=h, b=b: P.dve(
                            lambda e: e.scalar_tensor_tensor(out=ebv, in0=SELv[:, b, :], scalar=sc, in1=ebv, op0=ALU.mult, op1=ALU.add),
                            reads=[t_sel, t_Eb], writes=[t_EBh[h]]))

        def drain_bg(n):
            for _ in range(min(n, len(bg_dve))):
                bg_dve.pop(0)()

        W0 = A0 + 65536
        YT0 = 126976

        def ckpt(name):
            if stop_after == name:
                raise _Stop()

        try:
          ckpt("consts")
          for s in range(nseq):
              slotA = [ualloc(f"sA{i}_{s}", A0 + i * 8192, 8192, BF16) for i in range(4)]
              qbuf = [ualloc(f"qb{i}_{s}", A0 + 32768 + i * 4096, 4096, BF16) for i in range(2)]
              kbuf = [ualloc(f"kb{i}_{s}", A0 + 40960 + i * 4096, 4096, BF16) for i in range(2)]
              vbuf, t_vbuf = ualloc(f"vb_{s}", A0 + 49152, 16384, BF16)
              yT = [(uview(YT0 + i * 16384, 16384, BF16, "p (c n) -> p c n", c=4), None) for i in range(3)]
              t_yT = [[M.tok(f"yT{i}_{g}_{s}", YT0 + i * 16384, YT0 + (i + 1) * 16384) for g in range(4)] for i in range(3)]
              t_uT = [M.tok(f"uT{g}_{s}", 0, 32768) for g in range(4)]
              bankrr = [0]

              def nbank(lo=0, n=4):
                  b = lo + bankrr[0] % n
                  bankrr[0] += 1
                  return b

              def wslotA(i, K, ncols):
                  v, tk = slotA[i]
                  return v[:, 0:K * ncols].rearrange("p (k n) -> p k n", k=K), tk

              def stage_norm_T(src_dram_rows, stream, xt_v, xt_t, xs_v, xs_t, g_ap, g_tok, bank, dst_ap, dst_tok):
                  P.dma("sp", stream, lambda e: e.dma_start(out=xt_v, in_=src_dram_rows), writes=[xt_t])
                  rs, rtk = rstd_of(xt_v, [xt_t], junk[:])
                  P.dve(lambda e: e.scalar_tensor_tensor(out=xs_v, in0=xt_v, scalar=rs, in1=g_ap, op0=ALU.mult, op1=ALU.mult),
                        reads=[xt_t, rtk, g_tok], writes=[xs_t])
                  pt = psT(bank)
                  for k in range(8):
                      P.pe(lambda e, k=k: e.transpose(out=pt[:, k * 128:(k + 1) * 128], in_=xs_v[:, k * 128:(k + 1) * 128],
                                                      identity=IDENT), reads=[xs_t, t_cb], writes=[pb[bank]])
                  P.act(lambda e: e.copy(out=dst_ap, in_=pt.rearrange("p (k n) -> p k n", k=8)), reads=[pb[bank]], writes=[dst_tok])

              xt = [ualloc(f"xt{i}_{s}", W0 + i * 4096, 4096, F32) for i in range(4)]
              xs = [ualloc(f"xs{i}_{s}", W0 + 16384 + i * 2048, 2048, BF16) for i in range(4)]
              a_state = {}

              def a_st0(tb):
                  i = tb % 4
                  P.dma("sp", f"xa{i}", lambda e: e.dma_start(out=xt[i][0], in_=x[s, tb * 128:(tb + 1) * 128, :]), writes=[xt[i][1]])
                  stg, rs, rtk = rstd_stages(xt[i][0], [xt[i][1]], False)
                  a_state[tb] = (stg, rs, rtk)
                  stg[0]()

              def a_st1(tb):
                  stg, rs, rtk = a_state[tb]
                  stg[1]()
                  stg[2]()

              def a_st2(tb):
                  i = tb % 4
                  stg, rs, rtk = a_state[tb]
                  P.dve(lambda e: e.scalar_tensor_tensor(out=xs[i][0], in0=xt[i][0], scalar=rs, in1=gB[:, 0, :], op0=ALU.mult, op1=ALU.mult),
                        reads=[xt[i][1], rtk, t_gB[0]], writes=[xs[i][1]])
                  bank = 4 + i
                  pt = psT(bank)
                  for k in range(8):
                      P.pe(lambda e, k=k: e.transpose(out=pt[:, k * 128:(k + 1) * 128], in_=xs[i][0][:, k * 128:(k + 1) * 128],
                                                      identity=IDENT), reads=[xs[i][1], t_cb], writes=[pb[bank]])

              def a_st3(tb):
                  bank = 4 + tb % 4
                  P.act(lambda e: e.copy(out=uT[:, :, tb * 128:(tb + 1) * 128], in_=psT(bank).rearrange("p (k n) -> p k n", k=8)),
                        reads=[pb[bank]], writes=[t_uT[tb // 4]])

              ckpt("A")
              def proj_fm(wv, wt, col0, M_, dst_fn, dst_tok, scale, evac):
                  for tg in range(4):
                      b = nbank(0, 4)
                      for k in range(8):
                          mm(ps[0:M_, b, :], wv[:, k, col0:col0 + M_], uT[:, k, tg * 512:(tg + 1) * 512], k == 0, k == 7,
                             [wt, t_uT[tg]], [pb[b]])
                      dst = dst_fn(tg)
                      if evac == "act":
                          P.act(lambda e, dst=dst, b=b: e.activation(out=dst, in_=ps[0:M_, b, :], func=AF.Copy, scale=scale),
                                reads=[pb[b]], writes=[dst_tok])
                      else:
                          P.dve(lambda e, dst=dst, b=b: e.tensor_scalar(out=dst, in0=ps[0:M_, b, :], scalar1=scale, scalar2=None,
                                                                      op0=ALU.mult), reads=[pb[b]], writes=[dst_tok])

              def proj_tm(wv, wt, col0, N_, dst_fn, dst_tok):
                  for tb in range(NBLK):
                      b = nbank(0, 4)
                      for k in range(8):
                          mm(ps[:, b, 0:N_], uT[:, k, tb * 128:(tb + 1) * 128], wv[:, k, col0:col0 + N_], k == 0, k == 7,
                             [wt, t_uT[tb // 4]], [pb[b]])
                      dst = dst_fn(tb)
                      if tb % 2 == 0:
                          P.act(lambda e, dst=dst, b=b: e.copy(out=dst, in_=ps[:, b, 0:N_]), reads=[pb[b]], writes=[dst_tok])
                      else:
                          P.dve(lambda e, dst=dst, b=b: e.tensor_copy(out=dst, in_=ps[:, b, 0:N_]), reads=[pb[b]], writes=[dst_tok])

              def wsrc(c0, ncols):
                  return w_in[:, c0:c0 + ncols].rearrange("(k p) n -> p k n", p=128)

              wq_v, wq_t = wslotA(0, 8, 512)
              wk_v, wk_t = wslotA(1, 8, 512)
              wv_v, wv_t = wslotA(2, 8, 512)
              load_w("wA2", wv_v, wsrc(VB, 512), wv_t)
              deferred_loads = [lambda: load_w("wA0", wq_v, wsrc(QB, 512), wq_t, after=[t_uT[1]]),
                                lambda: load_w("wA1", wk_v, wsrc(KB, 512), wk_t, after=[t_uT[1]])]
              if s == 0:
                  deferred_loads.append(lambda: P.dma("pool", "sel", lambda e: e.dma_start(
                      out=SELv, in_=sel.rearrange("p (b n) -> p b n", b=32)), reads=[t_uT[2]], writes=[t_sel]))
              vb3 = vbuf.rearrange("p (b n) -> p b n", b=NBLK)
              def vproj_block(tb):
                  b = nbank(0, 4)
                  for k in range(8):
                      mm(ps[:, b, :], uT[:, k, tb * 128:(tb + 1) * 128], wv_v[:, k, :], k == 0, k == 7, [wv_t, t_uT[tb // 4]], [pb[b]])
                  if tb % 2 == 0:
                      P.act(lambda e: e.copy(out=vb3[:, tb, :], in_=ps[:, b, :]), reads=[pb[b]], writes=[t_vbuf])
                  else:
                      P.dve(lambda e: e.tensor_copy(out=vb3[:, tb, :], in_=ps[:, b, :]), reads=[pb[b]], writes=[t_vbuf])

              for t in range(NBLK + 4):
                  if t < NBLK:
                      a_st0(t)
                  if 0 <= t - 1 < NBLK:
                      a_st1(t - 1)
                  if 0 <= t - 2 < NBLK:
                      a_st2(t - 2)
                  if 0 <= t - 3 < NBLK:
                      a_st3(t - 3)
                  if 0 <= t - 4 < NBLK:
                      vproj_block(t - 4)
              for fn in deferred_loads:
                  fn()

              Ev = [ualloc(f"E{i}_{s}", W0 + i * 4096, 4096, F32, "p (j n) -> p j n", j=2) for i in range(2)]
              SPb = [ualloc(f"SP{i}_{s}", W0 + 8192 + i * 2048, 2048, BF16, "p (j n) -> p j n", j=2) for i in range(3)]
              LSb = [ualloc(f"LS{i}_{s}", W0 + 14336 + i * 2048, 2048, BF16, "p (j n) -> p j n", j=2) for i in range(4)]
              Ab = [ualloc(f"A{i}_{s}", W0 + 22528 + i * 2048, 2048, BF16, "p (j n) -> p j n", j=2) for i in range(2)]

              def sb_project(c):
                  qv, qt = qbuf[c % 2]
                  kv, kt = kbuf[c % 2]
                  proj_fm(wq_v, wq_t, c * 128, 128, lambda tg: qv[:, tg * 512:(tg + 1) * 512], qt, 0.125, "dve")
                  proj_fm(wk_v, wk_t, c * 128, 128, lambda tg: kv[:, tg * 512:(tg + 1) * 512], kt, 1.0, "dve")

              def sb_project_units(c):
                  qv, qt = qbuf[c % 2]
                  kv, kt = kbuf[c % 2]
                  units = []
                  for (wv_, wt_, dv_, dt_, sc_) in ((wq_v, wq_t, qv, qt, 0.125), (wk_v, wk_t, kv, kt, 1.0)):
                      for tg in range(4):
                          for k in range(8):
                              units.append(lambda wv_=wv_, wt_=wt_, tg=tg, k=k: mm(
                                  ps[:, 7, :], wv_[:, k, c * 128:(c + 1) * 128], uT[:, k, tg * 512:(tg + 1) * 512], k == 0, k == 7,
                                  [wt_, t_uT[tg]], [pb[7]]))
                          units.append(lambda dv_=dv_, dt_=dt_, sc_=sc_, tg=tg: P.dve(
                              lambda e: e.tensor_scalar(out=dv_[:, tg * 512:(tg + 1) * 512], in0=ps[:, 7, :], scalar1=sc_,
                                                        scalar2=None, op0=ALU.mult), reads=[pb[7]], writes=[dt_]))
                  return units

              zrr = [0]
              sprr = [0]
              arr = [0]
              orr = [0]

              def sb_attend(c, bg_units=()):
                  bg_units = list(bg_units)
                  qv, qt = qbuf[c % 2]
                  kv, kt = kbuf[c % 2]
                  steps = []
                  for g in range(4):
                      for kb in range(4 * g + 3, -1, -1):
                          steps.append((g, kb))
                  state = {}

                  def stage1(i):
                      g, kb = steps[i]
                      first = (kb == 4 * g + 3)
                      diag = kb >= 4 * g
                      lo = max(kb - 4 * g, 0) * 128
                      zb = 2 * (zrr[0] % 3)
                      zrr[0] += 1
                      spi = sprr[0] % 3
                      sprr[0] += 1
                      stt = dict(g=g, kb=kb, first=first, diag=diag, lo=lo, zb=zb, spi=spi)
                      state[i] = stt
                      if first:
                          ob = 6
                          orr[0] += 1
                          stt["ob"] = ob
                          for (lv, lt) in LSb[2 * (g % 2):2 * (g % 2) + 2]:
                              P.pool(lambda e, lv=lv: e.memset(lv, 0.0), writes=[lt])
                          stt["lsi"] = 0
                      else:
                          stt["ob"] = state[i - 1]["ob"]
                          stt["lsi"] = 1 - state[i - 1]["lsi"]
                      for j in range(2):
                          mm(ps[:, zb + j, lo:512], kv[j * 64:(j + 1) * 64, kb * 128:(kb + 1) * 128],
                             qv[j * 64:(j + 1) * 64, g * 512 + lo:(g + 1) * 512], True, True, [kt, qt], [pb[zb + j]])
                      zt = [pb[zb], pb[zb + 1]]
                      spv, spt = SPb[spi]
                      P.act(lambda e: e.activation(out=Ev[i % 2][0][:, :, lo:512], in_=ps[:, zb:zb + 2, lo:512], func=AF.Exp),
                            reads=zt, writes=[Ev[i % 2][1]])

                  def stage1b(i):
                      stt = state[i]
                      lo, diag = stt["lo"], stt["diag"]
                      spv, spt = SPb[stt["spi"]]
                      P.act(lambda e: e.activation(out=spv[:, :, lo:512], in_=Ev[i % 2][0][:, :, lo:512], func=AF.Ln, bias=1.0),
                            reads=[Ev[i % 2][1]], writes=[spt])
                      if diag:
                          for j in range(2):
                              P.pool(lambda e, j=j: e.tensor_tensor(out=spv[:, j, lo:lo + 128], in0=spv[:, j, lo:lo + 128], in1=CAUS,
                                                                    op=ALU.mult), reads=[spt, t_cb], writes=[spt])

                  def stage2(i):
                      stt = state[i]
                      g, kb, lo, zb, diag, first = stt["g"], stt["kb"], stt["lo"], stt["zb"], stt["diag"], stt["first"]
                      spv, spt = SPb[stt["spi"]]
                      lcur_v, lcur_t = LSb[2 * (g % 2) + stt["lsi"]]
                      lnxt_v, lnxt_t = LSb[2 * (g % 2) + 1 - stt["lsi"]]
                      lo_old = lo + 128 if diag else 0
                      for j in range(2):
                          mm(ps[:, zb + j, lo:512], NEGTRI, spv[:, j, lo:512], False, True, [t_cb, spt], [pb[zb + j]], skip=True)
                          if not first and lo_old < 512:
                              mm(ps[:, zb + j, lo_old:512], NEGONES, lcur_v[:, j, lo_old:512], False, True, [t_cb, lcur_t],
                                 [pb[zb + j]], skip=True)
                          if diag:
                              mm(ps[:, zb + j, lo:lo + 128], IDENT, NEGM, False, True, [t_cb], [pb[zb + j]], skip=True)
                      if kb > 0:
                          P.dve(lambda e: e.tensor_tensor(out=lnxt_v[:, :, lo:512], in0=lcur_v[:, :, lo:512], in1=spv[:, :, lo:512],
                                                          op=ALU.add), reads=[lcur_t, spt], writes=[lnxt_t])
                      drain_bg(2)
                      ai = arr[0] % 2
                      arr[0] += 1
                      stt["ai"] = ai
                      av, at = Ab[ai]
                      P.act(lambda e: e.activation(out=av[:, :, lo:512], in_=ps[:, zb:zb + 2, lo:512], func=AF.Exp),
                            reads=[pb[zb], pb[zb + 1]], writes=[at])

                  def stage3(i):
                      stt = state[i]
                      g, kb, lo, ob = stt["g"], stt["kb"], stt["lo"], stt["ob"]
                      av, at = Ab[stt["ai"]]
                      fst = stt["first"]
                      for j in range(2):
                          h = 2 * c + j
                          mm(ps[j * 64:(j + 1) * 64, ob, lo:512], vb3[:, kb, h * 64:(h + 1) * 64], av[:, j, lo:512], fst, True,
                             [t_vbuf, at], [pb[ob]], skip=not fst)
                      if kb == 0:
                          P.dve(lambda e: e.tensor_copy(out=yT[1][0][:, c, g * 512:(g + 1) * 512], in_=ps[:, ob, :]),
                                reads=[pb[ob]], writes=[t_yT[1][g]])

                  n = len(steps)
                  for i in range(n + 2):
                      if i < n:
                          stage1(i)
                          stage1b(i)
                      if 0 <= i - 1 < n:
                          stage2(i - 1)
                      if 0 <= i - 2 < n:
                          stage3(i - 2)
                      for _ in range(2):
                          if bg_units:
                              bg_units.pop(0)()
                  while bg_units:
                      bg_units.pop(0)()

              ckpt("sbv")
              sb_project(0)
              ckpt("sbproj")
              for c in range(4):
                  if c == 3:
                      sw_q = wslotA(3, 8, 512)
                      sw_kv = wslotA(0, 8, 384)
                      mm_k = wslotA(1, 8, 512)
                      mm_v = wslotA(2, 8, 512)
                      load_w("wA3", sw_q[0], wsrc(QA, 512), sw_q[1])
                      for jk_ in range(2):
                          for dup in range(2):
                              c0_ = jk_ * 128 + dup * 64
                              load_w("wA0", sw_kv[0][:, :, c0_:c0_ + 64], wsrc(KA + jk_ * 64, 64), sw_kv[1])
                      load_w("wA0", sw_kv[0][:, :, 256:384], wsrc(VA, 128), sw_kv[1])
                      load_w("wA1", mm_k[0], w_mem_kv[:, 0:512].rearrange("(k p) n -> p k n", p=128), mm_k[1])
                      load_w("wA2", mm_v[0], w_mem_kv[:, 512:1024].rearrange("(k p) n -> p k n", p=128), mm_v[1])
                  sb_attend(c, sb_project_units(c + 1) if c + 1 < 4 else ())

              ckpt("sb")
              drain_bg(10 ** 6)
              wq_v, wq_t = sw_q
              wkv_v, wkv_t = sw_kv
              kd = [ualloc(f"kd{i}_{s}", A0 + 40960 + i * 4096, 4096, BF16) for i in range(2)]
              va_v, va_t = ualloc(f"va_{s}", A0 + 49152, 4096, BF16, "p (b n) -> p b n", b=NBLK)
              proj_tm(wkv_v, wkv_t, 256, 128, lambda tb: va_v[:, tb, :], va_t)
              for jk in range(2):
                  kv_, kt_ = kd[jk]
                  for tg in range(4):
                      b = nbank(0, 4)
                      for k in range(8):
                          mm(ps[:, b, :], wkv_v[:, k, jk * 128:(jk + 1) * 128],
                             uT[:, k, tg * 512:(tg + 1) * 512], k == 0, k == 7, [wkv_t, t_uT[tg]], [pb[b]])
                      P.dve(lambda e, b=b, kv_=kv_, tg=tg: e.tensor_copy(out=kv_[:, tg * 512:(tg + 1) * 512], in_=ps[:, b, :]),
                            reads=[pb[b]], writes=[kt_])
              wqm_v, wqm_t = wslotA(0, 8, 512)
              load_w("wA0", wqm_v, wsrc(QM, 512), wqm_t)
              wmk_v, wmk_t = mm_k
              wmv_v, wmv_t = mm_v
              gM_v, gM_t = ualloc(f"gM_{s}", A0 + 49152 + 4096, 4096, F32)
              P.dma("sp", "gm", lambda e: e.dma_start(out=gM_v, in_=gains[4, :].partition_broadcast(128)), writes=[gM_t])
              xt = [ualloc(f"mxt{i}_{s}", W0 + 16384 + i * 4096, 4096, F32) for i in range(2)]
              xs = [ualloc(f"mxs{i}_{s}", W0 + 24576 + i * 2048, 2048, BF16) for i in range(2)]
              memT, t_memT = ualloc(f"memT_{s}", W0 + 8192, 4096, BF16, "p (k n) -> p k n", k=8)
              for mb in range(2):
                  stage_norm_T(mem[s, mb * 128:(mb + 1) * 128, :], f"xa{mb}", xt[mb][0], xt[mb][1], xs[mb][0], xs[mb][1],
                               gM_v, gM_t, 6 + mb, memT[:, :, mb * 128:(mb + 1) * 128], t_memT)
              mkT, t_mkT = ualloc(f"mkT_{s}", W0 + 12288, 2048, BF16, "p (h n) -> p h n", h=4)
              mv, t_mv = ualloc(f"mv_{s}", W0 + 14336, 2048, BF16, "p (b n) -> p b n", b=2)
              for h in range(4):
                  b = nbank(0, 4)
                  for k in range(8):
                      mm(ps[:, b, 0:256], wmk_v[:, k, h * 128:(h + 1) * 128], memT[:, k, :], k == 0, k == 7, [wmk_t, t_memT], [pb[b]])
                  P.act(lambda e, b=b, h=h: e.copy(out=mkT[:, h, :], in_=ps[:, b, 0:256]), reads=[pb[b]], writes=[t_mkT])
              for mb in range(2):
                  b = nbank(0, 4)
                  for k in range(8):
                      mm(ps[:, b, :], memT[:, k, mb * 128:(mb + 1) * 128], wmv_v[:, k, :], k == 0, k == 7, [wmv_t, t_memT], [pb[b]])
                  P.act(lambda e, b=b, mb=mb: e.copy(out=mv[:, mb, :], in_=ps[:, b, :]), reads=[pb[b]], writes=[t_mv])
              expS = [ualloc(f"xS{i}_{s}", W0 + i * 2048, 2048, F32) for i in range(2)]
              pTs = [ualloc(f"pTs{i}_{s}", W0 + 4096 + i * 1024, 1024, BF16) for i in range(2)]
              rdS = [ualloc(f"rdS{i}_{s}", W0 + 6144 + i * 512, 512, F32) for i in range(2)]
              for c in range(4):
                  qv, qt = qbuf[c % 2]
                  proj_fm(wq_v, wq_t, c * 128, 128, lambda tg: qv[:, tg * 512:(tg + 1) * 512], qt, 0.125, "act")
                  kvh = c // 2
                  kdv, kdt = kd[kvh]
                  items = list(range(NBLK))
                  stS = {}

                  def swa1(n):
                      sbk = 2 * (n % 2)
                      i2 = n % 2
                      ev, et = expS[i2]
                      pv, pt_ = pTs[i2]
                      kbis = (1,) if n == 0 else (0, 1)
                      for j in range(2):
                          for kbi in kbis:
                              kb = n - 1 + kbi
                              col = kbi * 128
                              mm(ps[:, sbk + j, col:col + 128], kdv[j * 64:(j + 1) * 64, kb * 128:(kb + 1) * 128],
                                 qv[j * 64:(j + 1) * 64, n * 128:(n + 1) * 128], True, True, [kdt, qt], [pb[sbk + j]])
                      c0 = 128 if n == 0 else 0
                      src = ps[:, sbk:sbk + 2, c0:256]
                      evv = ev.rearrange("p (j n) -> p j n", j=2)[:, :, c0:256]
                      ebv = EB[:, c, :].rearrange("p (j n) -> p j n", j=2)[:, :, c0:256]
                      pvv = pv.rearrange("p (j n) -> p j n", j=2)[:, :, c0:256]
                      P.act(lambda e: e.activation(out=evv, in_=src, func=AF.Exp), reads=[pb[sbk], pb[sbk + 1]], writes=[et])
                      P.pool(lambda e: e.tensor_tensor(out=pvv[:, 0, :], in0=evv[:, 0, :], in1=ebv[:, 0, :], op=ALU.mult),
                             reads=[et, t_EBh[2 * c]], writes=[pt_])
                      P.dve(lambda e: e.tensor_tensor(out=pvv[:, 1, :], in0=evv[:, 1, :], in1=ebv[:, 1, :], op=ALU.mult),
                            reads=[et, t_EBh[2 * c + 1]], writes=[pt_])
                      stS[n] = (i2, kbis)

                  def swa2(n):
                      i2, kbis = stS[n]
                      pv, pt_ = pTs[i2]
                      rv, rt = rdS[i2]
                      ob = 4 + (n % 2)
                      db = 6 + (n % 2)
                      for j in range(2):
                          for ii, kbi in enumerate(kbis):
                              kb = n - 1 + kbi
                              col = j * 256 + kbi * 128
                              mm(ps[j * 64:(j + 1) * 64, ob, 0:128], va_v[:, kb, kvh * 64:(kvh + 1) * 64], pv[:, col:col + 128],
                                 ii == 0, ii == len(kbis) - 1, [va_t, pt_], [pb[ob]])
                      for j in range(2):
                          for ii, kbi in enumerate(kbis):
                              col = j * 256 + kbi * 128
                              mm(ps[j * 64:(j + 1) * 64, db, 0:128], ONES[:, 0:64], pv[:, col:col + 128],
                                 ii == 0, ii == len(kbis) - 1, [t_cb, pt_], [pb[db]])
                      P.act(lambda e: e.activation(out=rv, in_=ps[:, db, 0:128], func=AF.Ln, bias=sinkexp[:, c:c + 1]),
                            reads=[pb[db], t_sink], writes=[rt])
                      P.act(lambda e: e.activation(out=rv, in_=rv, func=AF.Exp, scale=-1.0), reads=[rt], writes=[rt])
                      P.dve(lambda e: e.tensor_tensor(out=yT[0][0][:, c, n * 128:(n + 1) * 128], in0=ps[:, ob, 0:128], in1=rv,
                                                      op=ALU.mult), reads=[pb[ob], rt], writes=[t_yT[0][n // 4]])

                  for n in range(NBLK + 1):
                      if n < NBLK:
                          swa1(n)
                      if n >= 1:
                          swa2(n - 1)

              ckpt("swa")
              wmk_v, wmk_t = mm_k
              wmv_v, wmv_t = mm_v
              pre_T = {i: ualloc(f"sT{i}_{s}", A0 + i * 8192, 8192, BF16) for i in (1, 2, 3)}
              wbr = []
              for i in range(3):
                  v, tk = pre_T[i + 1]
                  v = v.rearrange("p (k n) -> p k n", k=4)
                  load_w(f"wT{i + 1}", v, w_b[i].rearrange("(k p) n -> p k n", p=128), tk)
                  wbr.append((v, tk))
              pTm = [ualloc(f"pTm{i}_{s}", W0, 2048, BF16, "p (b n) -> p b n", b=2) if i == 0 else
                     ualloc(f"pTm{i}_{s}", W0 + 2048, 2048, BF16, "p (b n) -> p b n", b=2) for i in range(2)]
              rdM = [ualloc(f"rdM{i}_{s}", W0 + 4096 + i * 2048, 2048, F32) for i in range(2)]
              mscale = 1.0 / math.sqrt(128.0)
              for h in range(4):
                  qv, qt = qbuf[h % 2]
                  proj_fm(wqm_v, wqm_t, h * 128, 128, lambda tg: qv[:, tg * 512:(tg + 1) * 512], qt, 1.0, "act")
                  stM = {}

                  def mem1(tg):
                      zb = 2 * (tg % 2)
                      i2 = tg % 2
                      pv, pt_ = pTm[i2]
                      for mb in range(2):
                          mm(ps[:, zb + mb, :], mkT[:, h, mb * 128:(mb + 1) * 128], qv[:, tg * 512:(tg + 1) * 512], True, True,
                             [t_mkT, qt], [pb[zb + mb]])
                      P.act(lambda e: e.activation(out=pv, in_=ps[:, zb:zb + 2, :], func=AF.Exp, scale=mscale),
                            reads=[pb[zb], pb[zb + 1]], writes=[pt_])

                  def mem2(tg):
                      i2 = tg % 2
                      pv, pt_ = pTm[i2]
                      rv, rt = rdM[i2]
                      ob = 4 + i2
                      db = 6 + i2
                      for mb in range(2):
                          mm(ps[:, ob, :], mv[:, mb, h * 128:(h + 1) * 128], pv[:, mb, :], mb == 0, mb == 1, [t_mv, pt_], [pb[ob]])
                      for mb in range(2):
                          mm(ps[:, db, :], ONES, pv[:, mb, :], mb == 0, mb == 1, [t_cb, pt_], [pb[db]])
                      P.act(lambda e: e.activation(out=rv, in_=ps[:, db, :], func=AF.Ln), reads=[pb[db]], writes=[rt])
                      P.act(lambda e: e.activation(out=rv, in_=rv, func=AF.Exp, scale=-1.0), reads=[rt], writes=[rt])
                      P.dve(lambda e: e.tensor_tensor(out=yT[2][0][:, h, tg * 512:(tg + 1) * 512], in0=ps[:, ob, :], in1=rv,
                                                      op=ALU.mult), reads=[pb[ob], rt], writes=[t_yT[2][tg]])

                  for tg in range(5):
                      if tg < 4:
                          mem1(tg)
                      if tg >= 1:
                          mem2(tg - 1)


              ckpt("mem")
              slotT = [pre_T[i] if i in pre_T else ualloc(f"sT{i}_{s}", A0 + i * 8192, 8192, BF16) for i in range(6)]
              gate_slots = (0, 4, 5)
              sig = [ualloc(f"sig{i}_{s}", A0 + 49152 + i * 2048, 2048, F32) for i in range(3)]
              prd = [ualloc(f"prd{i}_{s}", A0 + 49152 + 6144 + i * 2048, 2048, F32) for i in range(2)]
              MR0 = A0 + 59392
              mrgT = uview(MR0, 32768, BF16, "p (k n) -> p k n", k=8)
              t_mrg = [M.tok(f"mrg{g}_{s}", MR0, MR0 + 32768) for g in range(4)]
              prr = [0]
              pv0, pt0 = prd[0]
              pv1, pt1 = prd[1]
              for dcg in range(2):
                  wg = []
                  for i in range(3):
                      v, tk = slotT[gate_slots[i]]
                      v = v.rearrange("p (k n) -> p k n", k=8)
                      load_w(f"wT{gate_slots[i]}", v, wsrc(GL + i * 1024 + dcg * 512, 512), tk)
                      wg.append((v, tk))
                  for tg in range(4):
                      for dcl in range(4):
                          dc = dcg * 4 + dcl
                          for i in range(3):
                              pr = 2 * (prr[0] % 4)
                              prr[0] += 1
                              gb_, bb_ = pr, pr + 1
                              for k in range(8):
                                  mm(ps[:, gb_, :], wg[i][0][:, k, dcl * 128:(dcl + 1) * 128], uT[:, k, tg * 512:(tg + 1) * 512],
                                     k == 0, k == 7, [wg[i][1], t_uT[tg]], [pb[gb_]])
                              for k in range(4):
                                  mm(ps[:, bb_, :], wbr[i][0][:, k, dc * 128:(dc + 1) * 128], yT[i][0][:, k, tg * 512:(tg + 1) * 512],
                                     k == 0, k == 3, [wbr[i][1], t_yT[i][tg]], [pb[bb_]])
                              sv, stk = sig[i]
                              P.act(lambda e, sv=sv, gb_=gb_: e.activation(out=sv, in_=ps[:, gb_, :], func=AF.Sigmoid),
                                    reads=[pb[gb_]], writes=[stk])
                              if i == 0:
                                  P.dve(lambda e, sv=sv, bb_=bb_: e.tensor_tensor(out=pv0, in0=ps[:, bb_, :], in1=sv, op=ALU.mult),
                                        reads=[pb[bb_], stk], writes=[pt0])
                              elif i == 1:
                                  P.dve(lambda e, sv=sv, bb_=bb_: e.tensor_tensor(out=pv1, in0=ps[:, bb_, :], in1=sv, op=ALU.mult),
                                        reads=[pb[bb_], stk], writes=[pt1])
                                  P.pool(lambda e: e.tensor_tensor(out=pv0, in0=pv0, in1=pv1, op=ALU.add),
                                         reads=[pt0, pt1], writes=[pt0])
                              else:
                                  P.dve(lambda e, sv=sv, bb_=bb_: e.tensor_tensor(out=pv1, in0=ps[:, bb_, :], in1=sv, op=ALU.mult),
                                        reads=[pb[bb_], stk], writes=[pt1])
                                  P.pool(lambda e, dc=dc, tg=tg: e.tensor_tensor(out=mrgT[:, dc, tg * 512:(tg + 1) * 512], in0=pv0, in1=pv1,
                                                                                op=ALU.add), reads=[pt0, pt1], writes=[t_mrg[tg]])

              ckpt("t1a")
              wo_v, wo_t = ualloc(f"wo_{s}", 0, 16384, BF16, "p (k n) -> p k n", k=8)
              load_w("wo", wo_v, w_out.rearrange("(k p) n -> p k n", p=128), wo_t)
              NSF = 4
              slotF = [ualloc(f"sF{i}_{s}", 16384 + i * 8192, 8192, BF16) for i in range(NSF)]
              aT, t_aT = ualloc(f"aT_{s}", 49152, 22528, BF16, "p (f n) -> p f n", f=NFC)
              u2Tb = [ualloc(f"u2T{i}_{s}", 71680 + i * 8192, 8192, BF16, "p (k n) -> p k n", k=8) for i in range(2)]
              sgb = [ualloc(f"sg{i}_{s}", 88064 + i * 2048, 2048, F32) for i in range(2)]
              FB0 = 124928
              xhb = [uview(FB0 + p_ * 16384, 16384, F32, "p (b n) -> p b n", b=4) for p_ in range(2)]
              t_xhb = [[M.tok(f"xh{p_}_{b}_{s}", FB0 + p_ * 16384 + b * 4096, FB0 + p_ * 16384 + (b + 1) * 4096) for b in range(4)]
                       for p_ in range(2)]
              hsb = [ualloc(f"hs{i}_{s}", FB0 + 32768 + i * 2048, 2048, BF16) for i in range(4)]
              ost = [ualloc(f"ost{i}_{s}", FB0 + 40960 + i * 4096, 4096, F32) for i in range(2)]
              sfr = [0]

              def next_slot():
                  i = sfr[0] % NSF
                  sfr[0] += 1
                  return i

              def bfs(chains, skew=0):
                  items = []
                  for i, c in enumerate(chains):
                      for k, fn in enumerate(c):
                          items.append((k + i * skew, i, k, fn))
                  items.sort(key=lambda t: (t[0], t[1]))
                  for _, _, _, fn in items:
                      fn()

              gB1p = gB[:, 1, :].rearrange("p (a b) -> p a b", a=2)
              gB3p = gB[:, 3, :].rearrange("p (a b) -> p a b", a=2)
              pieces = [(p * 4, min(4, NFC - p * 4)) for p in range(6)]

              def x_load(tg):
                  par = tg % 2
                  for tb in range(4):
                      P.dma("sp", f"xh{par}{tb}", lambda e, tb=tb: e.dma_start(
                          out=xhb[par][:, tb, :], in_=x[s, tg * 512 + tb * 128:tg * 512 + (tb + 1) * 128, :]), writes=[t_xhb[par][tb]])

              def t1b_chain(tg, tb, zb):
                  par = tg % 2
                  t0 = tg * 512
                  xh, t_xh = xhb[par], t_xhb[par]
                  u2T, t_u2T = u2Tb[par]
                  zt = [pb[zb], pb[zb + 1]]
                  zp = ps[:, zb:zb + 2, :]
                  hv, ht = hsb[tb]
                  pt = psT(zb)
                  stg1, rs1, rt1 = rstd_stages(zp, zt, True)
                  stg2, rs2, rt2 = rstd_stages(xh[:, tb, :], [t_xh[tb]], False)
                  xhp = xh[:, tb, :].rearrange("p (a b) -> p a b", a=2)

                  def mix():
                      for half in range(2):
                          for k in range(8):
                              mm(ps[:, zb + half, :], mrgT[:, k, t0 + tb * 128:t0 + (tb + 1) * 128],
                                 wo_v[:, k, half * 512:(half + 1) * 512], k == 0, k == 7, [t_mrg[tg], wo_t], [pb[zb + half]])

                  def n1_():
                      P.dve(lambda e: e.tensor_tensor(out=zp, in0=zp, in1=gB1p, op=ALU.mult), reads=zt + [t_gB[1]], writes=zt)

                  def hadd():
                      P.dve(lambda e: e.scalar_tensor_tensor(out=xhp, in0=zp, scalar=rs1, in1=xhp, op0=ALU.mult, op1=ALU.add),
                            reads=zt + [rt1, t_xh[tb]], writes=[t_xh[tb]])

                  def hs_():
                      P.dve(lambda e: e.scalar_tensor_tensor(out=hv, in0=xh[:, tb, :], scalar=rs2, in1=gB[:, 2, :],
                                                             op0=ALU.mult, op1=ALU.mult),
                            reads=[t_xh[tb], rt2, t_gB[2]], writes=[ht])

                  def tr():
                      for k in range(8):
                          P.pe(lambda e, k=k: e.transpose(out=pt[:, k * 128:(k + 1) * 128], in_=hv[:, k * 128:(k + 1) * 128],
                                                          identity=IDENT), reads=[ht, t_cb], writes=[pb[zb]])

                  def cp():
                      P.act(lambda e: e.copy(out=u2T[:, :, tb * 128:(tb + 1) * 128], in_=pt.rearrange("p (k n) -> p k n", k=8)),
                            reads=[pb[zb]], writes=[t_u2T])

                  return [mix, stg1[0], lambda: (stg1[1](), stg1[2](), n1_()), hadd, stg2[0], lambda: (stg2[1](), stg2[2]()), hs_, tr, cp]

              def gate_up(tg, hooks, npairs):
                  u2T, t_u2T = u2Tb[tg % 2]
                  gurr = 0
                  for (fc0, nfc) in pieces:
                      ncols = nfc * 128
                      ig, iu = next_slot(), next_slot()
                      gv = slotF[ig][0][:, 0:8 * ncols].rearrange("p (k n) -> p k n", k=8)
                      uv = slotF[iu][0][:, 0:8 * ncols].rearrange("p (k n) -> p k n", k=8)
                      gt, ut = slotF[ig][1], slotF[iu][1]
                      load_w(f"wF{ig}", gv, w_gate[:, fc0 * 128:fc0 * 128 + ncols].rearrange("(k p) n -> p k n", p=128), gt)
                      load_w(f"wF{iu}", uv, w_up[:, fc0 * 128:fc0 * 128 + ncols].rearrange("(k p) n -> p k n", p=128), ut)
                      for fl in range(nfc):
                          fc = fc0 + fl
                          zb = 2 * (gurr % npairs)
                          gurr += 1
                          for k in range(8):
                              mm(ps[:, zb, :], gv[:, k, fl * 128:(fl + 1) * 128], u2T[:, k, :], k == 0, k == 7, [gt, t_u2T], [pb[zb]])
                          for k in range(8):
                              mm(ps[:, zb + 1, :], uv[:, k, fl * 128:(fl + 1) * 128], u2T[:, k, :], k == 0, k == 7, [ut, t_u2T], [pb[zb + 1]])
                          sv, stk = sgb[fc % 2]
                          P.act(lambda e, sv=sv, zb=zb: e.activation(out=sv, in_=ps[:, zb, :], func=AF.Silu), reads=[pb[zb]], writes=[stk])
                          P.dve(lambda e, sv=sv, zb=zb, fc=fc: e.tensor_tensor(out=aT[:, fc, :], in0=ps[:, zb + 1, :], in1=sv, op=ALU.mult),
                                reads=[pb[zb + 1], stk], writes=[t_aT])
                          for fn in hooks.get(fc, ()):
                              fn()

              def down(tg):
                  for (fc0, nfc) in pieces:
                      idn = next_slot()
                      dv = slotF[idn][0][:, 0:nfc * 1024].rearrange("p (f n) -> p f n", f=nfc)
                      dtk = slotF[idn][1]
                      load_w(f"wF{idn}", dv, w_down[fc0 * 128:(fc0 + nfc) * 128, :].rearrange("(f p) n -> p f n", p=128), dtk)
                      for fl in range(nfc):
                          fc = fc0 + fl
                          for tb in range(4):
                              for half in range(2):
                                  mm(ps[:, 2 * tb + half, :], aT[:, fc, tb * 128:(tb + 1) * 128], dv[:, fl, half * 512:(half + 1) * 512],
                                     fc == 0, fc == NFC - 1, [t_aT, dtk], [pb[2 * tb + half]])

              def fin_chain(tg, tb):
                  par = tg % 2
                  t0 = tg * 512
                  xh, t_xh = xhb[par], t_xhb[par]
                  zb = 2 * tb
                  zt = [pb[zb], pb[zb + 1]]
                  zp = ps[:, zb:zb + 2, :]
                  ov, ot = ost[tb % 2]
                  stg3, rs3, rt3 = rstd_stages(zp, zt, True)

                  def n3():
                      P.dve(lambda e: e.tensor_tensor(out=zp, in0=zp, in1=gB3p, op=ALU.mult), reads=zt + [t_gB[3]], writes=zt)

                  def oadd():
                      P.dve(lambda e: e.scalar_tensor_tensor(out=ov.rearrange("p (a b) -> p a b", a=2), in0=zp, scalar=rs3,
                                                             in1=xh[:, tb, :].rearrange("p (a b) -> p a b", a=2),
                                                             op0=ALU.mult, op1=ALU.add),
                            reads=zt + [rt3, t_xh[tb]], writes=[ot])

                  def odma():
                      P.dma("sp", f"o{tb % 2}", lambda e: e.dma_start(out=out[s, t0 + tb * 128:t0 + (tb + 1) * 128, :], in_=ov),
                            reads=[ot])

                  return [stg3[0], lambda: (stg3[1](), stg3[2](), n3()), lambda: (oadd(), odma())]

              x_load(0)
              bfs([t1b_chain(0, tb, 2 * tb) for tb in range(4)], skew=2)
              for tg in range(4):
                  hooks = {}
                  if tg + 1 < 4:
                      x_load(tg + 1)
                      for tb in range(4):
                          ch = t1b_chain(tg + 1, tb, 4 + 2 * (tb % 2))
                          base = 1 + 4 * tb
                          plan = {base: [ch[0]], base + 1: [ch[1], ch[2]], base + 2: [ch[3], ch[4]], base + 3: [ch[5], ch[6]],
                                  base + 4: [ch[7]], base + 5: [ch[8]]}
                          for fc_, fns in plan.items():
                              hooks.setdefault(fc_, []).extend(fns)
                  gate_up(tg, hooks, 2 if tg + 1 < 4 else 4)
                  down(tg)
                  bfs([fin_chain(tg, tb) for tb in range(4)], skew=1)

        except _Stop:
            pass

        P.emit(final_wait_streams=[f"o{tb}" for tb in range(4) if f"o{tb}" in P.dma_streams])
    return nc


_NC_CACHE = {}


def kernel(x, mem, ln_mix_pre, ln_mix_post, w_in, swa_sinks, rel_bias, ln_mem, w_mem_kv,
           w_branch_swa, w_branch_sb, w_branch_mem, w_out, ln_ffn_pre, ln_ffn_post,
           w_gate, w_up, w_down):
    f = lambda a: np.ascontiguousarray(np.asarray(a, dtype=np.float32))
    x = f(x)
    mem = f(mem)
    cst, sel = _host_consts()
    gains = np.concatenate([f(ln_mix_pre)[0:1], f(ln_mix_post)[0:1], f(ln_ffn_pre)[0:1], f(ln_ffn_post)[0:1],
                            f(ln_mem)[0:1]], axis=0)
    sinks = np.ascontiguousarray(f(swa_sinks)[0].reshape(4, 2).T)
    relb = f(rel_bias).reshape(1, 256)
    shared = {
        "w_in": f(w_in)[0], "w_mem_kv": f(w_mem_kv)[0], "w_bswa": f(w_branch_swa)[0], "w_bsb": f(w_branch_sb)[0],
        "w_bmem": f(w_branch_mem)[0], "w_out": f(w_out)[0], "w_gate": f(w_gate)[0], "w_up": f(w_up)[0],
        "w_down": f(w_down)[0], "gains": np.ascontiguousarray(gains), "sinks": sinks, "relb": relb,
        "cst": cst, "sel": sel,
    }
    nc = build_nc()
    in_maps = []
    for c in range(NCORES):
        m = dict(shared)
        m["x"] = x[c * NSEQ:(c + 1) * NSEQ]
        m["mem"] = mem[c * NSEQ:(c + 1) * NSEQ]
        in_maps.append(m)
    res = run_bass_kernel_spmd(nc, in_maps, core_ids=list(range(NCORES)))
    outs = [np.asarray(r["out"]) for r in res.results]
    return np.concatenate(outs, axis=0).astype(np.float32, copy=False)
```

```python
import math
import contextlib
import numpy as np
import concourse.bass as bass
import concourse.mybir as mybir
from concourse.bass_utils import run_bass_kernel_spmd

F32 = mybir.dt.float32
BF16 = mybir.dt.bfloat16
AF = mybir.ActivationFunctionType
ALU = mybir.AluOpType

NCORES = 8
NSEQ = 2
S = 2048
D = 1024
NBLK = 16
DFF = 2816
NFC = 22
MEM = 256
EPS = 1e-6
QA, KA, VA, QB, KB, VB, QM, GL = 0, 512, 640, 768, 1280, 1792, 2304, 2816

SAME_ENGINE_SYNC = True


class Tok:
    __slots__ = ("name", "w", "r")

    def __init__(self, name):
        self.name = name
        self.w = None
        self.r = []


class Op:
    __slots__ = ("id", "eng", "fn", "deps", "is_dma", "dsem", "dval", "needs_inc", "cnt")


class _Rec:
    def __init__(self):
        self.call = None

    def __getattr__(self, name):
        def f(*a, **k):
            self.call = (name, a, k)
            return self
        return f

    def replay(self, eng):
        name, a, k = self.call
        return getattr(eng, name)(*a, **k)


class Prog:
    ENGS = ("pe", "act", "dve", "pool", "sp")

    def __init__(self, nc):
        self.nc = nc
        self.ops = []
        self.by_eng = {e: [] for e in self.ENGS}
        self.dma_streams = {}

    def add(self, eng, fn, reads=(), writes=(), dma=None):
        op = Op()
        op.id = len(self.ops)
        op.eng = eng
        rec = _Rec()
        fn(rec)
        op.fn = rec.replay
        op.is_dma = dma is not None
        op.needs_inc = False
        op.cnt = 0
        deps = set()
        for t in reads:
            if t.w is not None:
                deps.add(t.w)
        for t in writes:
            if t.w is not None:
                deps.add(t.w)
            deps.update(t.r)
        op.deps = deps
        for t in reads:
            t.r.append(op.id)
        for t in writes:
            t.w = op.id
            t.r = []
        if dma is not None:
            if dma not in self.dma_streams:
                self.dma_streams[dma] = [len(self.dma_streams), 0]
            st = self.dma_streams[dma]
            st[1] += 16
            op.dsem = st[0]
            op.dval = st[1]
        self.ops.append(op)
        self.by_eng[eng].append(op)
        return op

    def pe(self, fn, reads=(), writes=()):
        return self.add("pe", fn, reads, writes)

    def act(self, fn, reads=(), writes=()):
        return self.add("act", fn, reads, writes)

    def dve(self, fn, reads=(), writes=()):
        return self.add("dve", fn, reads, writes)

    def pool(self, fn, reads=(), writes=()):
        return self.add("pool", fn, reads, writes)

    def dma(self, eng, stream, fn, reads=(), writes=()):
        return self.add(eng, fn, reads, writes, dma=stream)

    def emit(self, final_wait_streams=()):
        nc = self.nc
        ops = self.ops
        for op in ops:
            for d in op.deps:
                dop = ops[d]
                if dop.is_dma:
                    continue
                if dop.eng != op.eng or (SAME_ENGINE_SYNC and op.eng != "pe"):
                    dop.needs_inc = True
        for e in self.ENGS:
            c = 0
            for op in self.by_eng[e]:
                if op.needs_inc:
                    c += 1
                op.cnt = c
        with contextlib.ExitStack() as es:
            esem = {e: es.enter_context(nc.semaphore(f"s_{e}")) for e in self.ENGS}
            dsem = [es.enter_context(nc.semaphore(f"d_{i}")) for i in range(len(self.dma_streams))]
            block = es.enter_context(nc.Block())
            streams_final = [tuple(self.dma_streams[s]) for s in final_wait_streams]

            def make(e):
                def body(eng):
                    known = {}
                    for op in self.by_eng[e]:
                        waits = {}
                        for d in op.deps:
                            dop = ops[d]
                            if dop.is_dma:
                                key = ("d", dop.dsem)
                                val = dop.dval
                            else:
                                if dop.eng == e and (e == "pe" or not SAME_ENGINE_SYNC):
                                    continue
                                key = ("e", dop.eng)
                                val = dop.cnt
                            if waits.get(key, 0) < val:
                                waits[key] = val
                        for key, val in waits.items():
                            if known.get(key, 0) >= val:
                                continue
                            known[key] = val
                            sem = dsem[key[1]] if key[0] == "d" else esem[key[1]]
                            eng.wait_ge(sem, val)
                        inst = op.fn(eng)
                        if op.is_dma:
                            inst.then_inc(dsem[op.dsem], 16)
                        elif op.needs_inc:
                            inst.then_inc(esem[e], 1)
                    if e == "sp":
                        for (si, val) in streams_final:
                            eng.wait_ge(dsem[si], val)
                return body

            block.tensor(make("pe"))
            block.scalar(make("act"))
            block.vector(make("dve"))
            block.gpsimd(make("pool"))
            block.sync(make("sp"))


class _Stop(Exception):
    pass


class Mem:
    def __init__(self):
        self.reg = []

    def tok(self, name, lo, hi):
        t = Tok(name)
        seen = set()
        for (a, b, o) in self.reg:
            if a < hi and lo < b:
                if o.w is not None:
                    seen.add(o.w)
                seen.update(o.r)
        t.r = list(seen)
        self.reg.append((lo, hi, t))
        return t


def _t5_bucket_np(dist):
    max_exact = 16
    d = np.maximum(dist, 0)
    df = np.maximum(d, 1).astype(np.float32)
    large = max_exact + (np.log(df / np.float32(max_exact)) / np.float32(math.log(128 / max_exact))
                         * np.float32(32 - max_exact)).astype(np.int32)
    large = np.minimum(large, 31)
    return np.where(d < max_exact, d, large)


def _host_consts():
    k = np.arange(128)[:, None]
    q = np.arange(128)[None, :]
    ident = np.eye(128, dtype=np.float32)
    negtri = -(k >= q).astype(np.float32)
    negones = -np.ones((128, 128), np.float32)
    ones = np.ones((128, 128), np.float32)
    caus = (k < q).astype(np.float32)
    negm = np.where(k >= q, -30000.0, 0.0).astype(np.float32)
    cst = np.stack([ident, negtri, negones, ones, caus, negm], axis=1).reshape(128, 6 * 128)
    sel = np.zeros((128, 32, 2, 128), np.float32)
    for kbi in range(2):
        dist = (q + 128 - k) if kbi == 0 else (q - k)
        valid = (dist >= 0) & (dist < 128)
        bucket = _t5_bucket_np(dist)
        for b in range(32):
            sel[:, b, kbi, :] = (valid & (bucket == b)).astype(np.float32)
    return np.ascontiguousarray(cst), np.ascontiguousarray(sel.reshape(128, 32 * 256))


def build_nc(nseq=NSEQ, stop_after=None, dbg=None):
    nc = bass.Bass("TRN2", target_bir_lowering=False)

    def din(name, shape):
        return nc.dram_tensor(name, shape, F32, kind="ExternalInput").ap()

    x = din("x", [nseq, S, D])
    mem = din("mem", [nseq, MEM, D])
    w_in = din("w_in", [D, 5888])
    w_mem_kv = din("w_mem_kv", [D, 1024])
    w_b = [din("w_bswa", [512, D]), din("w_bsb", [512, D]), din("w_bmem", [512, D])]
    w_out = din("w_out", [D, D])
    w_gate = din("w_gate", [D, DFF])
    w_up = din("w_up", [D, DFF])
    w_down = din("w_down", [DFF, D])
    gains = din("gains", [5, D])
    sinks = din("sinks", [2, 4])
    relb = din("relb", [1, 256])
    cst = din("cst", [128, 768])
    sel = din("sel", [128, 8192])
    out = nc.dram_tensor("out", [nseq, S, D], F32, kind="ExternalOutput").ap()

    P = Prog(nc)
    M = Mem()
    NU = 176128 // 2
    A0 = 32768

    with contextlib.ExitStack() as es:
        def sb(name, shape, dt):
            return es.enter_context(nc.sbuf_tensor(name, shape, dt))

        cb = sb("cb", [128, 6, 128], BF16)
        gB = sb("gB", [128, 4, 1024], F32)
        EB = sb("EB", [128, 4, 512], F32)
        Eb = sb("Eb", [128, 256], F32)
        sinkexp = sb("sinkexp", [128, 4], F32)
        mhalf = sb("mhalf", [128, 1], F32)
        eps_t = sb("eps_t", [128, 1], F32)
        st = sb("st", [128, 512], F32)
        junk = sb("junk", [128, 1024], BF16)
        jk2 = sb("jk2", [128, 2, 1024], BF16)
        t_jk = [Tok("jk0"), Tok("jk1")]
        U = sb("U", [128, NU], BF16)
        ps = es.enter_context(nc.psum_tensor("ps", [128, 8, 512], F32))
        uT = U[:, 0:16384].rearrange("p (k n) -> p k n", k=8)

        t_cb, t_EB, t_Eb, t_sink, t_mhalf, t_junk = (Tok(n) for n in ("cb", "EB", "Eb", "sink", "mhalf", "junk"))
        t_gB = [Tok(f"gB{i}") for i in range(4)]
        pb = [Tok(f"pb{i}") for i in range(8)]
        IDENT, NEGTRI, NEGONES, ONES, CAUS, NEGM = (cb[:, i, :] for i in range(6))
        statcol = [0]

        def uview(lo, nbytes, dt, pattern=None, **kw):
            ap = U[:, lo // 2:(lo + nbytes) // 2]
            if dt == F32:
                ap = ap.bitcast(F32)
            if pattern is not None:
                ap = ap.rearrange(pattern, **kw)
            return ap

        def ualloc(name, lo, nbytes, dt, pattern=None, **kw):
            return uview(lo, nbytes, dt, pattern, **kw), M.tok(name, lo, lo + nbytes)

        def psT(bank):
            return ps[:, bank, :].bitcast(BF16)

        def mm(out_ap, lhsT, rhs, start, stop, reads, writes, skip=False):
            P.pe(lambda e: e.matmul(out_ap, lhsT=lhsT, rhs=rhs, start=start, stop=stop, skip_group_check=skip),
                 reads, writes)

        def load_w(stream, dst_view, src_ap, tok, after=()):
            P.dma("pool", stream, lambda e: e.dma_start(out=dst_view, in_=src_ap), reads=list(after), writes=[tok])

        def rstd_of(src_ap, src_toks, junk_view):
            c = statcol[0]
            statcol[0] += 3
            tk = Tok(f"st{c}")
            P.act(lambda e: e.activation(out=junk_view, in_=src_ap, func=AF.Square, accum_out=st[:, c:c + 1]),
                  reads=src_toks, writes=[tk, t_junk])
            P.act(lambda e: e.activation(out=st[:, c + 1:c + 2], in_=st[:, c:c + 1], func=AF.Ln, scale=1.0 / D, bias=eps_t[:, 0:1]),
                  reads=[tk, t_mhalf], writes=[tk])
            P.act(lambda e: e.activation(out=st[:, c + 2:c + 3], in_=st[:, c + 1:c + 2], func=AF.Exp, scale=-0.5),
                  reads=[tk], writes=[tk])
            return st[:, c + 2:c + 3], tk

        junk2 = junk[:].rearrange("p (a b) -> p a b", a=2)
        jrr = [0]

        def rstd_stages(src_ap, src_toks, paired):
            c = statcol[0]
            statcol[0] += 3
            tk = Tok(f"st{c}")
            ji = jrr[0] % 2
            jrr[0] += 1
            jv = jk2[:, ji, :]
            if paired:
                jv = jv.rearrange("p (a b) -> p a b", a=2)

            def a():
                P.act(lambda e: e.activation(out=jv, in_=src_ap, func=AF.Square, accum_out=st[:, c:c + 1]),
                      reads=src_toks, writes=[tk, t_jk[ji]])

            def b():
                P.act(lambda e: e.activation(out=st[:, c + 1:c + 2], in_=st[:, c:c + 1], func=AF.Ln, scale=1.0 / D, bias=eps_t[:, 0:1]),
                      reads=[tk, t_mhalf], writes=[tk])

            def cc():
                P.act(lambda e: e.activation(out=st[:, c + 2:c + 3], in_=st[:, c + 1:c + 2], func=AF.Exp, scale=-0.5),
                      reads=[tk], writes=[tk])

            return [a, b, cc], st[:, c + 2:c + 3], tk

        P.dma("pool", "cst", lambda e: e.dma_start(out=cb[:].rearrange("p a b -> p (a b)"), in_=cst), writes=[t_cb])
        for i in range(4):
            P.dma("act", f"g{i}", lambda e, i=i: e.dma_start(out=gB[:, i, :], in_=gains[i, :].partition_broadcast(128)),
                  writes=[t_gB[i]])
        P.dma("act", "relb", lambda e: e.dma_start(out=Eb[:], in_=relb[0, :].partition_broadcast(128)), writes=[t_Eb])
        for j in range(2):
            P.dma("act", f"sink{j}", lambda e, j=j: e.dma_start(out=sinkexp[j * 64:(j + 1) * 64, :],
                                                               in_=sinks[j, :].partition_broadcast(64)), writes=[t_sink])
        SELv, t_sel = ualloc("sel", 126976, 16384, BF16, "p (b n) -> p b n", b=32)
        P.pool(lambda e: e.memset(eps_t[:], EPS), writes=[t_mhalf])
        P.act(lambda e: e.activation(out=Eb[:], in_=Eb[:], func=AF.Exp), reads=[t_Eb], writes=[t_Eb])
        P.act(lambda e: e.activation(out=sinkexp[:], in_=sinkexp[:], func=AF.Exp), reads=[t_sink], writes=[t_sink])
        t_EBh = [Tok(f"EBh{h}") for h in range(8)]
        bg_dve = []
        for b in range(32):
            for c in range(4):
                for j in range(2):
                    h = 2 * c + j
                    ebv = EB[:, c, j * 256:(j + 1) * 256]
                    sc = Eb[:, b * 8 + h:b * 8 + h + 1]
                    if b == 0:
                        bg_dve.append(lambda ebv=ebv, sc=sc, h=h: P.dve(
                            lambda e: e.tensor_scalar(out=ebv, in0=SELv[:, 0, :], scalar1=sc, scalar2=None, op0=ALU.mult),
                            reads=[t_sel, t_Eb], writes=[t_EBh[h]]))
                    else:
                        bg_dve.append(lambda ebv=ebv, sc=sc, h=h, b=b: P.dve(
                            lambda e: e.scalar_tensor_tensor(out=ebv, in0=SELv[:, b, :], scalar=sc, in1=ebv, op0=ALU.mult, op1=ALU.add),
                            reads=[t_sel, t_Eb], writes=[t_EBh[h]]))

        def drain_bg(n):
            for _ in range(min(n, len(bg_dve))):
                bg_dve.pop(0)()

        W0 = A0 + 65536
        YT0 = 126976

        def ckpt(name):
            if stop_after == name:
                raise _Stop()

        try:
          ckpt("consts")
          for s in range(nseq):
              slotA = [ualloc(f"sA{i}_{s}", A0 + i * 8192, 8192, BF16) for i in range(4)]
              qbuf = [ualloc(f"qb{i}_{s}", A0 + 32768 + i * 4096, 4096, BF16) for i in range(2)]
              kbuf = [ualloc(f"kb{i}_{s}", A0 + 40960 + i * 4096, 4096, BF16) for i in range(2)]
              vbuf, t_vbuf = ualloc(f"vb_{s}", A0 + 49152, 16384, BF16)
              yT = [(uview(YT0 + i * 16384, 16384, BF16, "p (c n) -> p c n", c=4), None) for i in range(3)]
              t_yT = [[M.tok(f"yT{i}_{g}_{s}", YT0 + i * 16384, YT0 + (i + 1) * 16384) for g in range(4)] for i in range(3)]
              t_uT = [M.tok(f"uT{g}_{s}", 0, 32768) for g in range(4)]
              bankrr = [0]

              def nbank(lo=0, n=4):
                  b = lo + bankrr[0] % n
                  bankrr[0] += 1
                  return b

              def wslotA(i, K, ncols):
                  v, tk = slotA[i]
                  return v[:, 0:K * ncols].rearrange("p (k n) -> p k n", k=K), tk

              def stage_norm_T(src_dram_rows, stream, xt_v, xt_t, xs_v, xs_t, g_ap, g_tok, bank, dst_ap, dst_tok):
                  P.dma("sp", stream, lambda e: e.dma_start(out=xt_v, in_=src_dram_rows), writes=[xt_t])
                  rs, rtk = rstd_of(xt_v, [xt_t], junk[:])
                  P.dve(lambda e: e.scalar_tensor_tensor(out=xs_v, in0=xt_v, scalar=rs, in1=g_ap, op0=ALU.mult, op1=ALU.mult),
                        reads=[xt_t, rtk, g_tok], writes=[xs_t])
                  pt = psT(bank)
                  for k in range(8):
                      P.pe(lambda e, k=k: e.transpose(out=pt[:, k * 128:(k + 1) * 128], in_=xs_v[:, k * 128:(k + 1) * 128],
                                                      identity=IDENT), reads=[xs_t, t_cb], writes=[pb[bank]])
                  P.act(lambda e: e.copy(out=dst_ap, in_=pt.rearrange("p (k n) -> p k n", k=8)), reads=[pb[bank]], writes=[dst_tok])

              xt = [ualloc(f"xt{i}_{s}", W0 + i * 4096, 4096, F32) for i in range(4)]
              xs = [ualloc(f"xs{i}_{s}", W0 + 16384 + i * 2048, 2048, BF16) for i in range(4)]
              a_state = {}

              def a_st0(tb):
                  i = tb % 4
                  P.dma("sp", f"xa{i}", lambda e: e.dma_start(out=xt[i][0], in_=x[s, tb * 128:(tb + 1) * 128, :]), writes=[xt[i][1]])
                  stg, rs, rtk = rstd_stages(xt[i][0], [xt[i][1]], False)
                  a_state[tb] = (stg, rs, rtk)
                  stg[0]()

              def a_st1(tb):
                  stg, rs, rtk = a_state[tb]
                  stg[1]()
                  stg[2]()

              def a_st2(tb):
                  i = tb % 4
                  stg, rs, rtk = a_state[tb]
                  P.dve(lambda e: e.scalar_tensor_tensor(out=xs[i][0], in0=xt[i][0], scalar=rs, in1=gB[:, 0, :], op0=ALU.mult, op1=ALU.mult),
                        reads=[xt[i][1], rtk, t_gB[0]], writes=[xs[i][1]])
                  bank = 4 + i
                  pt = psT(bank)
                  for k in range(8):
                      P.pe(lambda e, k=k: e.transpose(out=pt[:, k * 128:(k + 1) * 128], in_=xs[i][0][:, k * 128:(k + 1) * 128],
                                                      identity=IDENT), reads=[xs[i][1], t_cb], writes=[pb[bank]])

              def a_st3(tb):
                  bank = 4 + tb % 4
                  if tb % 2 == 0:
                      P.act(lambda e: e.copy(out=uT[:, :, tb * 128:(tb + 1) * 128], in_=psT(bank).rearrange("p (k n) -> p k n", k=8)),
                            reads=[pb[bank]], writes=[t_uT[tb // 4]])
                  else:
                      P.dve(lambda e: e.tensor_copy(out=uT[:, :, tb * 128:(tb + 1) * 128],
                                                    in_=psT(bank).rearrange("p (k n) -> p k n", k=8)),
                            reads=[pb[bank]], writes=[t_uT[tb // 4]])

              ckpt("A")
              def proj_fm(wv, wt, col0, M_, dst_fn, dst_tok, scale, evac):
                  for tg in range(4):
                      b = nbank(0, 4)
                      for k in range(8):
                          mm(ps[0:M_, b, :], wv[:, k, col0:col0 + M_], uT[:, k, tg * 512:(tg + 1) * 512], k == 0, k == 7,
                             [wt, t_uT[tg]], [pb[b]])
                      dst = dst_fn(tg)
                      if evac == "act":
                          P.act(lambda e, dst=dst, b=b: e.activation(out=dst, in_=ps[0:M_, b, :], func=AF.Copy, scale=scale),
                                reads=[pb[b]], writes=[dst_tok])
                      else:
                          P.dve(lambda e, dst=dst, b=b: e.tensor_scalar(out=dst, in0=ps[0:M_, b, :], scalar1=scale, scalar2=None,
                                                                      op0=ALU.mult), reads=[pb[b]], writes=[dst_tok])

              def proj_tm(wv, wt, col0, N_, dst_fn, dst_tok):
                  for tb in range(NBLK):
                      b = nbank(0, 4)
                      for k in range(8):
                          mm(ps[:, b, 0:N_], uT[:, k, tb * 128:(tb + 1) * 128], wv[:, k, col0:col0 + N_], k == 0, k == 7,
                             [wt, t_uT[tb // 4]], [pb[b]])
                      dst = dst_fn(tb)
                      if tb % 2 == 0:
                          P.act(lambda e, dst=dst, b=b: e.copy(out=dst, in_=ps[:, b, 0:N_]), reads=[pb[b]], writes=[dst_tok])
                      else:
                          P.dve(lambda e, dst=dst, b=b: e.tensor_copy(out=dst, in_=ps[:, b, 0:N_]), reads=[pb[b]], writes=[dst_tok])

              def wsrc(c0, ncols):
                  return w_in[:, c0:c0 + ncols].rearrange("(k p) n -> p k n", p=128)

              wq_v, wq_t = wslotA(0, 8, 512)
              wk_v, wk_t = wslotA(1, 8, 512)
              wv_v, wv_t = wslotA(2, 8, 512)
              load_w("wA2", wv_v, wsrc(VB, 512), wv_t)
              deferred_loads = [lambda: load_w("wA0", wq_v, wsrc(QB, 512), wq_t, after=[t_uT[1]]),
                                lambda: load_w("wA1", wk_v, wsrc(KB, 512), wk_t, after=[t_uT[1]])]
              if s == 0:
                  deferred_loads.append(lambda: P.dma("pool", "sel", lambda e: e.dma_start(
                      out=SELv, in_=sel.rearrange("p (b n) -> p b n", b=32)), reads=[t_uT[2]], writes=[t_sel]))
              vb3 = vbuf.rearrange("p (b n) -> p b n", b=NBLK)
              def vproj_block(tb):
                  b = nbank(0, 4)
                  for k in range(8):
                      mm(ps[:, b, :], uT[:, k, tb * 128:(tb + 1) * 128], wv_v[:, k, :], k == 0, k == 7, [wv_t, t_uT[tb // 4]], [pb[b]])
                  if tb % 2 == 0:
                      P.act(lambda e: e.copy(out=vb3[:, tb, :], in_=ps[:, b, :]), reads=[pb[b]], writes=[t_vbuf])
                  else:
                      P.dve(lambda e: e.tensor_copy(out=vb3[:, tb, :], in_=ps[:, b, :]), reads=[pb[b]], writes=[t_vbuf])

              for t in range(NBLK + 4):
                  if t < NBLK:
                      a_st0(t)
                  if 0 <= t - 1 < NBLK:
                      a_st1(t - 1)
                  if 0 <= t - 2 < NBLK:
                      a_st2(t - 2)
                  if 0 <= t - 3 < NBLK:
                      a_st3(t - 3)
                  if 0 <= t - 4 < NBLK:
                      vproj_block(t - 4)
              for fn in deferred_loads:
                  fn()

              Ev = [ualloc(f"E{i}_{s}", W0 + i * 4096, 4096, F32, "p (j n) -> p j n", j=2) for i in range(2)]
              SPb = [ualloc(f"SP{i}_{s}", W0 + 8192 + i * 2048, 2048, BF16, "p (j n) -> p j n", j=2) for i in range(3)]
              LSb = [ualloc(f"LS{i}_{s}", W0 + 14336 + i * 2048, 2048, BF16, "p (j n) -> p j n", j=2) for i in range(4)]
              Ab = [ualloc(f"A{i}_{s}", W0 + 22528 + i * 2048, 2048, BF16, "p (j n) -> p j n", j=2) for i in range(2)]

              def sb_project(c):
                  qv, qt = qbuf[c % 2]
                  kv, kt = kbuf[c % 2]
                  proj_fm(wq_v, wq_t, c * 128, 128, lambda tg: qv[:, tg * 512:(tg + 1) * 512], qt, 0.125, "dve")
                  proj_fm(wk_v, wk_t, c * 128, 128, lambda tg: kv[:, tg * 512:(tg + 1) * 512], kt, 1.0, "dve")

              def sb_project_units(c):
                  qv, qt = qbuf[c % 2]
                  kv, kt = kbuf[c % 2]
                  units = []
                  for (wv_, wt_, dv_, dt_, sc_) in ((wq_v, wq_t, qv, qt, 0.125), (wk_v, wk_t, kv, kt, 1.0)):
                      for tg in range(4):
                          for k in range(8):
                              units.append(lambda wv_=wv_, wt_=wt_, tg=tg, k=k: mm(
                                  ps[:, 7, :], wv_[:, k, c * 128:(c + 1) * 128], uT[:, k, tg * 512:(tg + 1) * 512], k == 0, k == 7,
                                  [wt_, t_uT[tg]], [pb[7]]))
                          units.append(lambda dv_=dv_, dt_=dt_, sc_=sc_, tg=tg: P.dve(
                              lambda e: e.tensor_scalar(out=dv_[:, tg * 512:(tg + 1) * 512], in0=ps[:, 7, :], scalar1=sc_,
                                                        scalar2=None, op0=ALU.mult), reads=[pb[7]], writes=[dt_]))
                  return units

              zrr = [0]
              sprr = [0]
              arr = [0]
              orr = [0]

              def sb_attend(c, bg_units=()):
                  bg_units = list(bg_units)
                  qv, qt = qbuf[c % 2]
                  kv, kt = kbuf[c % 2]
                  steps = []
                  for g in range(4):
                      for kb in range(4 * g + 3, -1, -1):
                          steps.append((g, kb))
                  state = {}

                  def stage1(i):
                      g, kb = steps[i]
                      first = (kb == 4 * g + 3)
                      diag = kb >= 4 * g
                      lo = max(kb - 4 * g, 0) * 128
                      zb = 2 * (zrr[0] % 3)
                      zrr[0] += 1
                      spi = sprr[0] % 3
                      sprr[0] += 1
                      stt = dict(g=g, kb=kb, first=first, diag=diag, lo=lo, zb=zb, spi=spi)
                      state[i] = stt
                      if first:
                          ob = 6
                          orr[0] += 1
                          stt["ob"] = ob
                          for (lv, lt) in LSb[2 * (g % 2):2 * (g % 2) + 2]:
                              P.pool(lambda e, lv=lv: e.memset(lv, 0.0), writes=[lt])
                          stt["lsi"] = 0
                      else:
                          stt["ob"] = state[i - 1]["ob"]
                          stt["lsi"] = 1 - state[i - 1]["lsi"]
                      for j in range(2):
                          mm(ps[:, zb + j, lo:512], kv[j * 64:(j + 1) * 64, kb * 128:(kb + 1) * 128],
                             qv[j * 64:(j + 1) * 64, g * 512 + lo:(g + 1) * 512], True, True, [kt, qt], [pb[zb + j]])
                      zt = [pb[zb], pb[zb + 1]]
                      spv, spt = SPb[spi]
                      P.act(lambda e: e.activation(out=Ev[i % 2][0][:, :, lo:512], in_=ps[:, zb:zb + 2, lo:512], func=AF.Exp),
                            reads=zt, writes=[Ev[i % 2][1]])

                  def stage1b(i):
                      stt = state[i]
                      lo, diag = stt["lo"], stt["diag"]
                      spv, spt = SPb[stt["spi"]]
                      P.act(lambda e: e.activation(out=spv[:, :, lo:512], in_=Ev[i % 2][0][:, :, lo:512], func=AF.Ln, bias=1.0),
                            reads=[Ev[i % 2][1]], writes=[spt])
                      if diag:
                          for j in range(2):
                              P.pool(lambda e, j=j: e.tensor_tensor(out=spv[:, j, lo:lo + 128], in0=spv[:, j, lo:lo + 128], in1=CAUS,
                                                                    op=ALU.mult), reads=[spt, t_cb], writes=[spt])

                  def stage2(i):
                      stt = state[i]
                      g, kb, lo, zb, diag, first = stt["g"], stt["kb"], stt["lo"], stt["zb"], stt["diag"], stt["first"]
                      spv, spt = SPb[stt["spi"]]
                      lcur_v, lcur_t = LSb[2 * (g % 2) + stt["lsi"]]
                      lnxt_v, lnxt_t = LSb[2 * (g % 2) + 1 - stt["lsi"]]
                      lo_old = lo + 128 if diag else 0
                      for j in range(2):
                          mm(ps[:, zb + j, lo:512], NEGTRI, spv[:, j, lo:512], False, True, [t_cb, spt], [pb[zb + j]], skip=True)
                          if not first and lo_old < 512:
                              mm(ps[:, zb + j, lo_old:512], NEGONES, lcur_v[:, j, lo_old:512], False, True, [t_cb, lcur_t],
                                 [pb[zb + j]], skip=True)
                          if diag:
                              mm(ps[:, zb + j, lo:lo + 128], IDENT, NEGM, False, True, [t_cb], [pb[zb + j]], skip=True)
                      if kb > 0:
                          P.dve(lambda e: e.tensor_tensor(out=lnxt_v[:, :, lo:512], in0=lcur_v[:, :, lo:512], in1=spv[:, :, lo:512],
                                                          op=ALU.add), reads=[lcur_t, spt], writes=[lnxt_t])
                      drain_bg(2)
                      ai = arr[0] % 2
                      arr[0] += 1
                      stt["ai"] = ai
                      av, at = Ab[ai]
                      P.act(lambda e: e.activation(out=av[:, :, lo:512], in_=ps[:, zb:zb + 2, lo:512], func=AF.Exp),
                            reads=[pb[zb], pb[zb + 1]], writes=[at])

                  def stage3(i):
                      stt = state[i]
                      g, kb, lo, ob = stt["g"], stt["kb"], stt["lo"], stt["ob"]
                      av, at = Ab[stt["ai"]]
                      fst = stt["first"]
                      for j in range(2):
                          h = 2 * c + j
                          mm(ps[j * 64:(j + 1) * 64, ob, lo:512], vb3[:, kb, h * 64:(h + 1) * 64], av[:, j, lo:512], fst, True,
                             [t_vbuf, at], [pb[ob]], skip=not fst)
                      if kb == 0:
                          P.dve(lambda e: e.tensor_copy(out=yT[1][0][:, c, g * 512:(g + 1) * 512], in_=ps[:, ob, :]),
                                reads=[pb[ob]], writes=[t_yT[1][g]])

                  n = len(steps)
                  for i in range(n + 2):
                      if i < n:
                          stage1(i)
                          stage1b(i)
                      if 0 <= i - 1 < n:
                          stage2(i - 1)
                      if 0 <= i - 2 < n:
                          stage3(i - 2)
                      for _ in range(2):
                          if bg_units:
                              bg_units.pop(0)()
                  while bg_units:
                      bg_units.pop(0)()

              ckpt("sbv")
              sb_project(0)
              ckpt("sbproj")
              for c in range(4):
                  if c == 3:
                      sw_q = wslotA(3, 8, 512)
                      sw_kv = wslotA(0, 8, 384)
                      mm_k = wslotA(1, 8, 512)
                      mm_v = wslotA(2, 8, 512)
                      load_w("wA3", sw_q[0], wsrc(QA, 512), sw_q[1])
                      for jk_ in range(2):
                          for dup in range(2):
                              c0_ = jk_ * 128 + dup * 64
                              load_w("wA0", sw_kv[0][:, :, c0_:c0_ + 64], wsrc(KA + jk_ * 64, 64), sw_kv[1])
                      load_w("wA0", sw_kv[0][:, :, 256:384], wsrc(VA, 128), sw_kv[1])
                      load_w("wA1", mm_k[0], w_mem_kv[:, 0:512].rearrange("(k p) n -> p k n", p=128), mm_k[1])
                      load_w("wA2", mm_v[0], w_mem_kv[:, 512:1024].rearrange("(k p) n -> p k n", p=128), mm_v[1])
                  sb_attend(c, sb_project_units(c + 1) if c + 1 < 4 else ())

              ckpt("sb")
              drain_bg(10 ** 6)
              wq_v, wq_t = sw_q
              wkv_v, wkv_t = sw_kv
              kd = [ualloc(f"kd{i}_{s}", A0 + 40960 + i * 4096, 4096, BF16) for i in range(2)]
              va_v, va_t = ualloc(f"va_{s}", A0 + 49152, 4096, BF16, "p (b n) -> p b n", b=NBLK)
              proj_tm(wkv_v, wkv_t, 256, 128, lambda tb: va_v[:, tb, :], va_t)
              for jk in range(2):
                  kv_, kt_ = kd[jk]
                  for tg in range(4):
                      b = nbank(0, 4)
                      for k in range(8):
                          mm(ps[:, b, :], wkv_v[:, k, jk * 128:(jk + 1) * 128],
                             uT[:, k, tg * 512:(tg + 1) * 512], k == 0, k == 7, [wkv_t, t_uT[tg]], [pb[b]])
                      P.dve(lambda e, b=b, kv_=kv_, tg=tg: e.tensor_copy(out=kv_[:, tg * 512:(tg + 1) * 512], in_=ps[:, b, :]),
                            reads=[pb[b]], writes=[kt_])
              wqm_v, wqm_t = wslotA(0, 8, 512)
              load_w("wA0", wqm_v, wsrc(QM, 512), wqm_t)
              wmk_v, wmk_t = mm_k
              wmv_v, wmv_t = mm_v
              gM_v, gM_t = ualloc(f"gM_{s}", A0 + 49152 + 4096, 4096, F32)
              P.dma("sp", "gm", lambda e: e.dma_start(out=gM_v, in_=gains[4, :].partition_broadcast(128)), writes=[gM_t])
              xt = [ualloc(f"mxt{i}_{s}", W0 + 16384 + i * 4096, 4096, F32) for i in range(2)]
              xs = [ualloc(f"mxs{i}_{s}", W0 + 24576 + i * 2048, 2048, BF16) for i in range(2)]
              memT, t_memT = ualloc(f"memT_{s}", W0 + 8192, 4096, BF16, "p (k n) -> p k n", k=8)
              for mb in range(2):
                  stage_norm_T(mem[s, mb * 128:(mb + 1) * 128, :], f"xa{mb}", xt[mb][0], xt[mb][1], xs[mb][0], xs[mb][1],
                               gM_v, gM_t, 6 + mb, memT[:, :, mb * 128:(mb + 1) * 128], t_memT)
              mkT, t_mkT = ualloc(f"mkT_{s}", W0 + 12288, 2048, BF16, "p (h n) -> p h n", h=4)
              mv, t_mv = ualloc(f"mv_{s}", W0 + 14336, 2048, BF16, "p (b n) -> p b n", b=2)
              for h in range(4):
                  b = nbank(0, 4)
                  for k in range(8):
                      mm(ps[:, b, 0:256], wmk_v[:, k, h * 128:(h + 1) * 128], memT[:, k, :], k == 0, k == 7, [wmk_t, t_memT], [pb[b]])
                  P.act(lambda e, b=b, h=h: e.copy(out=mkT[:, h, :], in_=ps[:, b, 0:256]), reads=[pb[b]], writes=[t_mkT])
              for mb in range(2):
                  b = nbank(0, 4)
                  for k in range(8):
                      mm(ps[:, b, :], memT[:, k, mb * 128:(mb + 1) * 128], wmv_v[:, k, :], k == 0, k == 7, [wmv_t, t_memT], [pb[b]])
                  P.act(lambda e, b=b, mb=mb: e.copy(out=mv[:, mb, :], in_=ps[:, b, :]), reads=[pb[b]], writes=[t_mv])
              expS = [ualloc(f"xS{i}_{s}", W0 + i * 2048, 2048, F32) for i in range(2)]
              pTs = [ualloc(f"pTs{i}_{s}", W0 + 4096 + i * 1024, 1024, BF16) for i in range(2)]
              rdS = [ualloc(f"rdS{i}_{s}", W0 + 6144 + i * 512, 512, F32) for i in range(2)]
              for c in range(4):
                  qv, qt = qbuf[c % 2]
                  proj_fm(wq_v, wq_t, c * 128, 128, lambda tg: qv[:, tg * 512:(tg + 1) * 512], qt, 0.125, "act")
                  kvh = c // 2
                  kdv, kdt = kd[kvh]
                  items = list(range(NBLK))
                  stS = {}

                  def swa1(n):
                      sbk = 2 * (n % 2)
                      i2 = n % 2
                      ev, et = expS[i2]
                      pv, pt_ = pTs[i2]
                      kbis = (1,) if n == 0 else (0, 1)
                      for j in range(2):
                          for kbi in kbis:
                              kb = n - 1 + kbi
                              col = kbi * 128
                              mm(ps[:, sbk + j, col:col + 128], kdv[j * 64:(j + 1) * 64, kb * 128:(kb + 1) * 128],
                                 qv[j * 64:(j + 1) * 64, n * 128:(n + 1) * 128], True, True, [kdt, qt], [pb[sbk + j]])
                      c0 = 128 if n == 0 else 0
                      src = ps[:, sbk:sbk + 2, c0:256]
                      evv = ev.rearrange("p (j n) -> p j n", j=2)[:, :, c0:256]
                      ebv = EB[:, c, :].rearrange("p (j n) -> p j n", j=2)[:, :, c0:256]
                      pvv = pv.rearrange("p (j n) -> p j n", j=2)[:, :, c0:256]
                      P.act(lambda e: e.activation(out=evv, in_=src, func=AF.Exp), reads=[pb[sbk], pb[sbk + 1]], writes=[et])
                      P.pool(lambda e: e.tensor_tensor(out=pvv[:, 0, :], in0=evv[:, 0, :], in1=ebv[:, 0, :], op=ALU.mult),
                             reads=[et, t_EBh[2 * c]], writes=[pt_])
                      P.dve(lambda e: e.tensor_tensor(out=pvv[:, 1, :], in0=evv[:, 1, :], in1=ebv[:, 1, :], op=ALU.mult),
                            reads=[et, t_EBh[2 * c + 1]], writes=[pt_])
                      stS[n] = (i2, kbis)

                  def swa2(n):
                      i2, kbis = stS[n]
                      pv, pt_ = pTs[i2]
                      rv, rt = rdS[i2]
                      ob = 4 + (n % 2)
                      db = 6 + (n % 2)
                      for j in range(2):
                          for ii, kbi in enumerate(kbis):
                              kb = n - 1 + kbi
                              col = j * 256 + kbi * 128
                              mm(ps[j * 64:(j + 1) * 64, ob, 0:128], va_v[:, kb, kvh * 64:(kvh + 1) * 64], pv[:, col:col + 128],
                                 ii == 0, ii == len(kbis) - 1, [va_t, pt_], [pb[ob]])
                      for j in range(2):
                          for ii, kbi in enumerate(kbis):
                              col = j * 256 + kbi * 128
                              mm(ps[j * 64:(j + 1) * 64, db, 0:128], ONES[:, 0:64], pv[:, col:col + 128],
                                 ii == 0, ii == len(kbis) - 1, [t_cb, pt_], [pb[db]])
                      P.act(lambda e: e.activation(out=rv, in_=ps[:, db, 0:128], func=AF.Ln, bias=sinkexp[:, c:c + 1]),
                            reads=[pb[db], t_sink], writes=[rt])
                      P.act(lambda e: e.activation(out=rv, in_=rv, func=AF.Exp, scale=-1.0), reads=[rt], writes=[rt])
                      P.dve(lambda e: e.tensor_tensor(out=yT[0][0][:, c, n * 128:(n + 1) * 128], in0=ps[:, ob, 0:128], in1=rv,
                                                      op=ALU.mult), reads=[pb[ob], rt], writes=[t_yT[0][n // 4]])

                  for n in range(NBLK + 1):
                      if n < NBLK:
                          swa1(n)
                      if n >= 1:
                          swa2(n - 1)

              ckpt("swa")
              wmk_v, wmk_t = mm_k
              wmv_v, wmv_t = mm_v
              pre_T = {i: ualloc(f"sT{i}_{s}", A0 + i * 8192, 8192, BF16) for i in (1, 2, 3)}
              wbr = []
              for i in range(3):
                  v, tk = pre_T[i + 1]
                  v = v.rearrange("p (k n) -> p k n", k=4)
                  load_w(f"wT{i + 1}", v, w_b[i].rearrange("(k p) n -> p k n", p=128), tk)
                  wbr.append((v, tk))
              pTm = [ualloc(f"pTm{i}_{s}", W0, 2048, BF16, "p (b n) -> p b n", b=2) if i == 0 else
                     ualloc(f"pTm{i}_{s}", W0 + 2048, 2048, BF16, "p (b n) -> p b n", b=2) for i in range(2)]
              rdM = [ualloc(f"rdM{i}_{s}", W0 + 4096 + i * 2048, 2048, F32) for i in range(2)]
              mscale = 1.0 / math.sqrt(128.0)
              for h in range(4):
                  qv, qt = qbuf[h % 2]
                  proj_fm(wqm_v, wqm_t, h * 128, 128, lambda tg: qv[:, tg * 512:(tg + 1) * 512], qt, 1.0, "act")
                  stM = {}

                  def mem1(tg):
                      zb = 2 * (tg % 2)
                      i2 = tg % 2
                      pv, pt_ = pTm[i2]
                      for mb in range(2):
                          mm(ps[:, zb + mb, :], mkT[:, h, mb * 128:(mb + 1) * 128], qv[:, tg * 512:(tg + 1) * 512], True, True,
                             [t_mkT, qt], [pb[zb + mb]])
                      P.act(lambda e: e.activation(out=pv, in_=ps[:, zb:zb + 2, :], func=AF.Exp, scale=mscale),
                            reads=[pb[zb], pb[zb + 1]], writes=[pt_])

                  def mem2(tg):
                      i2 = tg % 2
                      pv, pt_ = pTm[i2]
                      rv, rt = rdM[i2]
                      ob = 4 + i2
                      db = 6 + i2
                      for mb in range(2):
                          mm(ps[:, ob, :], mv[:, mb, h * 128:(h + 1) * 128], pv[:, mb, :], mb == 0, mb == 1, [t_mv, pt_], [pb[ob]])
                      for mb in range(2):
                          mm(ps[:, db, :], ONES, pv[:, mb, :], mb == 0, mb == 1, [t_cb, pt_], [pb[db]])
                      P.act(lambda e: e.activation(out=rv, in_=ps[:, db, :], func=AF.Ln), reads=[pb[db]], writes=[rt])
                      P.act(lambda e: e.activation(out=rv, in_=rv, func=AF.Exp, scale=-1.0), reads=[rt], writes=[rt])
                      P.dve(lambda e: e.tensor_tensor(out=yT[2][0][:, h, tg * 512:(tg + 1) * 512], in0=ps[:, ob, :], in1=rv,
                                                      op=ALU.mult), reads=[pb[ob], rt], writes=[t_yT[2][tg]])

                  for tg in range(5):
                      if tg < 4:
                          mem1(tg)
                      if tg >= 1:
                          mem2(tg - 1)


              ckpt("mem")
              slotT = [pre_T[i] if i in pre_T else ualloc(f"sT{i}_{s}", A0 + i * 8192, 8192, BF16) for i in range(6)]
              gate_slots = (0, 4, 5)
              sig = [ualloc(f"sig{i}_{s}", A0 + 49152 + i * 2048, 2048, F32) for i in range(3)]
              prd = [ualloc(f"prd{i}_{s}", A0 + 49152 + 6144 + i * 2048, 2048, F32) for i in range(2)]
              MR0 = A0 + 59392
              mrgT = uview(MR0, 32768, BF16, "p (k n) -> p k n", k=8)
              t_mrg = [M.tok(f"mrg{g}_{s}", MR0, MR0 + 32768) for g in range(4)]
              prr = [0]
              pv0, pt0 = prd[0]
              pv1, pt1 = prd[1]
              for dcg in range(2):
                  wg = []
                  for i in range(3):
                      v, tk = slotT[gate_slots[i]]
                      v = v.rearrange("p (k n) -> p k n", k=8)
                      load_w(f"wT{gate_slots[i]}", v, wsrc(GL + i * 1024 + dcg * 512, 512), tk)
                      wg.append((v, tk))
                  for tg in range(4):
                      for dcl in range(4):
                          dc = dcg * 4 + dcl
                          for i in range(3):
                              pr = 2 * (prr[0] % 4)
                              prr[0] += 1
                              gb_, bb_ = pr, pr + 1
                              for k in range(8):
                                  mm(ps[:, gb_, :], wg[i][0][:, k, dcl * 128:(dcl + 1) * 128], uT[:, k, tg * 512:(tg + 1) * 512],
                                     k == 0, k == 7, [wg[i][1], t_uT[tg]], [pb[gb_]])
                              for k in range(4):
                                  mm(ps[:, bb_, :], wbr[i][0][:, k, dc * 128:(dc + 1) * 128], yT[i][0][:, k, tg * 512:(tg + 1) * 512],
                                     k == 0, k == 3, [wbr[i][1], t_yT[i][tg]], [pb[bb_]])
                              sv, stk = sig[i]
                              P.act(lambda e, sv=sv, gb_=gb_: e.activation(out=sv, in_=ps[:, gb_, :], func=AF.Sigmoid),
                                    reads=[pb[gb_]], writes=[stk])
                              if i == 0:
                                  P.dve(lambda e, sv=sv, bb_=bb_: e.tensor_tensor(out=pv0, in0=ps[:, bb_, :], in1=sv, op=ALU.mult),
                                        reads=[pb[bb_], stk], writes=[pt0])
                              elif i == 1:
                                  P.dve(lambda e, sv=sv, bb_=bb_: e.tensor_tensor(out=pv1, in0=ps[:, bb_, :], in1=sv, op=ALU.mult),
                                        reads=[pb[bb_], stk], writes=[pt1])
                                  P.pool(lambda e: e.tensor_tensor(out=pv0, in0=pv0, in1=pv1, op=ALU.add),
                                         reads=[pt0, pt1], writes=[pt0])
                              else:
                                  P.dve(lambda e, sv=sv, bb_=bb_: e.tensor_tensor(out=pv1, in0=ps[:, bb_, :], in1=sv, op=ALU.mult),
                                        reads=[pb[bb_], stk], writes=[pt1])
                                  P.pool(lambda e, dc=dc, tg=tg: e.tensor_tensor(out=mrgT[:, dc, tg * 512:(tg + 1) * 512], in0=pv0, in1=pv1,
                                                                                op=ALU.add), reads=[pt0, pt1], writes=[t_mrg[tg]])

              ckpt("t1a")
              wo_v, wo_t = ualloc(f"wo_{s}", 0, 16384, BF16, "p (k n) -> p k n", k=8)
              load_w("wo", wo_v, w_out.rearrange("(k p) n -> p k n", p=128), wo_t)
              NSF = 4
              slotF = [ualloc(f"sF{i}_{s}", 16384 + i * 8192, 8192, BF16) for i in range(NSF)]
              aT, t_aT = ualloc(f"aT_{s}", 49152, 22528, BF16, "p (f n) -> p f n", f=NFC)
              u2Tb = [ualloc(f"u2T{i}_{s}", 71680 + i * 8192, 8192, BF16, "p (k n) -> p k n", k=8) for i in range(2)]
              sgb = [ualloc(f"sg{i}_{s}", 88064 + i * 2048, 2048, F32) for i in range(2)]
              FB0 = 124928
              xhb = [uview(FB0 + p_ * 16384, 16384, F32, "p (b n) -> p b n", b=4) for p_ in range(2)]
              t_xhb = [[M.tok(f"xh{p_}_{b}_{s}", FB0 + p_ * 16384 + b * 4096, FB0 + p_ * 16384 + (b + 1) * 4096) for b in range(4)]
                       for p_ in range(2)]
              hsb = [ualloc(f"hs{i}_{s}", FB0 + 32768 + i * 2048, 2048, BF16) for i in range(4)]
              ost = [ualloc(f"ost{i}_{s}", FB0 + 40960 + i * 4096, 4096, F32) for i in range(2)]
              sfr = [0]

              def next_slot():
                  i = sfr[0] % NSF
                  sfr[0] += 1
                  return i

              def bfs(chains, skew=0):
                  items = []
                  for i, c in enumerate(chains):
                      for k, fn in enumerate(c):
                          items.append((k + i * skew, i, k, fn))
                  items.sort(key=lambda t: (t[0], t[1]))
                  for _, _, _, fn in items:
                      fn()

              gB1p = gB[:, 1, :].rearrange("p (a b) -> p a b", a=2)
              gB3p = gB[:, 3, :].rearrange("p (a b) -> p a b", a=2)
              pieces = [(p * 4, min(4, NFC - p * 4)) for p in range(6)]

              def x_load(tg):
                  par = tg % 2
                  for tb in range(4):
                      P.dma("sp", f"xh{par}{tb}", lambda e, tb=tb: e.dma_start(
                          out=xhb[par][:, tb, :], in_=x[s, tg * 512 + tb * 128:tg * 512 + (tb + 1) * 128, :]), writes=[t_xhb[par][tb]])

              def t1b_chain(tg, tb, zb):
                  par = tg % 2
                  t0 = tg * 512
                  xh, t_xh = xhb[par], t_xhb[par]
                  u2T, t_u2T = u2Tb[par]
                  zt = [pb[zb], pb[zb + 1]]
                  zp = ps[:, zb:zb + 2, :]
                  hv, ht = hsb[tb]
                  pt = psT(zb)
                  stg1, rs1, rt1 = rstd_stages(zp, zt, True)
                  stg2, rs2, rt2 = rstd_stages(xh[:, tb, :], [t_xh[tb]], False)
                  xhp = xh[:, tb, :].rearrange("p (a b) -> p a b", a=2)

                  def mix():
                      for half in range(2):
                          for k in range(8):
                              mm(ps[:, zb + half, :], mrgT[:, k, t0 + tb * 128:t0 + (tb + 1) * 128],
                                 wo_v[:, k, half * 512:(half + 1) * 512], k == 0, k == 7, [t_mrg[tg], wo_t], [pb[zb + half]])

                  def n1_():
                      P.dve(lambda e: e.tensor_tensor(out=zp, in0=zp, in1=gB1p, op=ALU.mult), reads=zt + [t_gB[1]], writes=zt)

                  def hadd():
                      P.dve(lambda e: e.scalar_tensor_tensor(out=xhp, in0=zp, scalar=rs1, in1=xhp, op0=ALU.mult, op1=ALU.add),
                            reads=zt + [rt1, t_xh[tb]], writes=[t_xh[tb]])

                  def hs_():
                      P.dve(lambda e: e.scalar_tensor_tensor(out=hv, in0=xh[:, tb, :], scalar=rs2, in1=gB[:, 2, :],
                                                             op0=ALU.mult, op1=ALU.mult),
                            reads=[t_xh[tb], rt2, t_gB[2]], writes=[ht])

                  def tr():
                      for k in range(8):
                          P.pe(lambda e, k=k: e.transpose(out=pt[:, k * 128:(k + 1) * 128], in_=hv[:, k * 128:(k + 1) * 128],
                                                          identity=IDENT), reads=[ht, t_cb], writes=[pb[zb]])

                  def cp():
                      P.act(lambda e: e.copy(out=u2T[:, :, tb * 128:(tb + 1) * 128], in_=pt.rearrange("p (k n) -> p k n", k=8)),
                            reads=[pb[zb]], writes=[t_u2T])

                  return [mix, stg1[0], lambda: (stg1[1](), stg1[2](), n1_()), hadd, stg2[0], lambda: (stg2[1](), stg2[2]()), hs_, tr, cp]

              def gate_up(tg, hooks, npairs):
                  u2T, t_u2T = u2Tb[tg % 2]
                  gurr = 0
                  for (fc0, nfc) in pieces:
                      ncols = nfc * 128
                      ig, iu = next_slot(), next_slot()
                      gv = slotF[ig][0][:, 0:8 * ncols].rearrange("p (k n) -> p k n", k=8)
                      uv = slotF[iu][0][:, 0:8 * ncols].rearrange("p (k n) -> p k n", k=8)
                      gt, ut = slotF[ig][1], slotF[iu][1]
                      load_w(f"wF{ig}", gv, w_gate[:, fc0 * 128:fc0 * 128 + ncols].rearrange("(k p) n -> p k n", p=128), gt)
                      load_w(f"wF{iu}", uv, w_up[:, fc0 * 128:fc0 * 128 + ncols].rearrange("(k p) n -> p k n", p=128), ut)
                      for fl in range(nfc):
                          fc = fc0 + fl
                          zb = 2 * (gurr % npairs)
                          gurr += 1
                          for k in range(8):
                              mm(ps[:, zb, :], gv[:, k, fl * 128:(fl + 1) * 128], u2T[:, k, :], k == 0, k == 7, [gt, t_u2T], [pb[zb]])
                          for k in range(8):
                              mm(ps[:, zb + 1, :], uv[:, k, fl * 128:(fl + 1) * 128], u2T[:, k, :], k == 0, k == 7, [ut, t_u2T], [pb[zb + 1]])
                          sv, stk = sgb[fc % 2]
                          P.act(lambda e, sv=sv, zb=zb: e.activation(out=sv, in_=ps[:, zb, :], func=AF.Silu), reads=[pb[zb]], writes=[stk])
                          P.dve(lambda e, sv=sv, zb=zb, fc=fc: e.tensor_tensor(out=aT[:, fc, :], in0=ps[:, zb + 1, :], in1=sv, op=ALU.mult),
                                reads=[pb[zb + 1], stk], writes=[t_aT])
                          for fn in hooks.get(fc, ()):
                              fn()

              def down(tg):
                  for (fc0, nfc) in pieces:
                      idn = next_slot()
                      dv = slotF[idn][0][:, 0:nfc * 1024].rearrange("p (f n) -> p f n", f=nfc)
                      dtk = slotF[idn][1]
                      load_w(f"wF{idn}", dv, w_down[fc0 * 128:(fc0 + nfc) * 128, :].rearrange("(f p) n -> p f n", p=128), dtk)
                      for fl in range(nfc):
                          fc = fc0 + fl
                          for tb in range(4):
                              for half in range(2):
                                  mm(ps[:, 2 * tb + half, :], aT[:, fc, tb * 128:(tb + 1) * 128], dv[:, fl, half * 512:(half + 1) * 512],
                                     fc == 0, fc == NFC - 1, [t_aT, dtk], [pb[2 * tb + half]])

              def fin_chain(tg, tb):
                  par = tg % 2
                  t0 = tg * 512
                  xh, t_xh = xhb[par], t_xhb[par]
                  zb = 2 * tb
                  zt = [pb[zb], pb[zb + 1]]
                  zp = ps[:, zb:zb + 2, :]
                  ov, ot = ost[tb % 2]
                  stg3, rs3, rt3 = rstd_stages(zp, zt, True)

                  def n3():
                      P.dve(lambda e: e.tensor_tensor(out=zp, in0=zp, in1=gB3p, op=ALU.mult), reads=zt + [t_gB[3]], writes=zt)

                  def oadd():
                      P.dve(lambda e: e.scalar_tensor_tensor(out=ov.rearrange("p (a b) -> p a b", a=2), in0=zp, scalar=rs3,
                                                             in1=xh[:, tb, :].rearrange("p (a b) -> p a b", a=2),
                                                             op0=ALU.mult, op1=ALU.add),
                            reads=zt + [rt3, t_xh[tb]], writes=[ot])

                  def odma():
                      P.dma("sp", f"o{tb % 2}", lambda e: e.dma_start(out=out[s, t0 + tb * 128:t0 + (tb + 1) * 128, :], in_=ov),
                            reads=[ot])

                  return [stg3[0], lambda: (stg3[1](), stg3[2](), n3()), lambda: (oadd(), odma())]

              x_load(0)
              bfs([t1b_chain(0, tb, 2 * tb) for tb in range(4)], skew=2)
              for tg in range(4):
                  hooks = {}
                  if tg + 1 < 4:
                      x_load(tg + 1)
                      for tb in range(4):
                          ch = t1b_chain(tg + 1, tb, 4 + 2 * (tb % 2))
                          base = 1 + 4 * tb
                          plan = {base: [ch[0]], base + 1: [ch[1], ch[2]], base + 2: [ch[3], ch[4]], base + 3: [ch[5], ch[6]],
                                  base + 4: [ch[7]], base + 5: [ch[8]]}
                          for fc_, fns in plan.items():
                              hooks.setdefault(fc_, []).extend(fns)
                  gate_up(tg, hooks, 2 if tg + 1 < 4 else 4)
                  down(tg)
                  bfs([fin_chain(tg, tb) for tb in range(4)], skew=1)

        except _Stop:
            pass

        P.emit(final_wait_streams=[f"o{tb}" for tb in range(4) if f"o{tb}" in P.dma_streams])
    return nc


_NC_CACHE = {}


def kernel(x, mem, ln_mix_pre, ln_mix_post, w_in, swa_sinks, rel_bias, ln_mem, w_mem_kv,
           w_branch_swa, w_branch_sb, w_branch_mem, w_out, ln_ffn_pre, ln_ffn_post,
           w_gate, w_up, w_down):
    f = lambda a: np.ascontiguousarray(np.asarray(a, dtype=np.float32))
    x = f(x)
    mem = f(mem)
    cst, sel = _host_consts()
    gains = np.concatenate([f(ln_mix_pre)[0:1], f(ln_mix_post)[0:1], f(ln_ffn_pre)[0:1], f(ln_ffn_post)[0:1],
                            f(ln_mem)[0:1]], axis=0)
    sinks = np.ascontiguousarray(f(swa_sinks)[0].reshape(4, 2).T)
    relb = f(rel_bias).reshape(1, 256)
    shared = {
        "w_in": f(w_in)[0], "w_mem_kv": f(w_mem_kv)[0], "w_bswa": f(w_branch_swa)[0], "w_bsb": f(w_branch_sb)[0],
        "w_bmem": f(w_branch_mem)[0], "w_out": f(w_out)[0], "w_gate": f(w_gate)[0], "w_up": f(w_up)[0],
        "w_down": f(w_down)[0], "gains": np.ascontiguousarray(gains), "sinks": sinks, "relb": relb,
        "cst": cst, "sel": sel,
    }
    nc = build_nc()
    in_maps = []
    for c in range(NCORES):
        m = dict(shared)
        m["x"] = x[c * NSEQ:(c + 1) * NSEQ]
        m["mem"] = mem[c * NSEQ:(c + 1) * NSEQ]
        in_maps.append(m)
    res = run_bass_kernel_spmd(nc, in_maps, core_ids=list(range(NCORES)))
    outs = [np.asarray(r["out"]) for r in res.results]
    return np.concatenate(outs, axis=0).astype(np.float32, copy=False)
```
